# Optimizing a Trainium2 kernel written in Bass

```python
import math
import jax, jax.numpy as jnp
from jax import lax
import numpy as np

D_MODEL = 2048
BATCH = 4
SEQ = 2048
DEPTH = 2

CHUNK = 64
EPS = 1e-6
ATT_HEADS = 8
ATT_HEAD_DIM = 128
ATT_WIDTH = ATT_HEADS * ATT_HEAD_DIM
ATT_LEFT_CHUNKS = 8
ATT_BAND = ATT_LEFT_CHUNKS + 1
REL_CLIP = 256
SSM_GROUP = 16
SSM_GROUPS = 48
SSM_WIDTH = SSM_GROUP * SSM_GROUPS
SSM_STATE = 64
SSM_DT_MIN = 1e-3
SSM_DT_MAX = 1e-1
MLSTM_HEADS = 4
MLSTM_HEAD_DIM = 256
MLSTM_WIDTH = MLSTM_HEADS * MLSTM_HEAD_DIM
MLSTM_CONV = 4
MLSTM_NEG = -1e30
N_BRANCH = 3
MEM_TOKENS = 256
MEM_HEADS = 4
MEM_HEAD_DIM = D_MODEL // MEM_HEADS
FFN_HIDDEN = -(-8 * D_MODEL // (3 * 256)) * 256
IN_WIDTH = 3 * ATT_WIDTH + SSM_WIDTH + 3 * MLSTM_WIDTH + 2 * MLSTM_HEADS + N_BRANCH * D_MODEL

kernel_name = "hybrid_gated_chunkattn_s5_mlstm_block"


def _in_split_points():
    widths = (ATT_WIDTH, ATT_WIDTH, ATT_WIDTH, SSM_WIDTH, MLSTM_WIDTH, MLSTM_WIDTH,
              MLSTM_WIDTH, MLSTM_HEADS, MLSTM_HEADS)
    pts, acc = [], 0
    for w in widths:
        acc += w
        pts.append(acc)
    return pts


def _rms(x, g):
    xf = x.astype(jnp.float32)
    y = xf * lax.rsqrt(jnp.mean(xf * xf, axis=-1, keepdims=True) + EPS)
    return (y * g.astype(jnp.float32)).astype(x.dtype)


def _chunk_attention(q, k, v, g_q, g_k, rel_bias):
    b, l, _ = q.shape
    nc = l // CHUNK
    shp = (b, nc, CHUNK, ATT_HEADS, ATT_HEAD_DIM)
    q = _rms(q.reshape(shp), g_q)
    k = _rms(k.reshape(shp), g_k)
    v = v.reshape(shp)
    pad = ((0, 0), (ATT_LEFT_CHUNKS, 0), (0, 0), (0, 0), (0, 0))
    kp, vp = jnp.pad(k, pad), jnp.pad(v, pad)
    kb = jnp.concatenate([kp[:, w:w + nc] for w in range(ATT_BAND)], axis=2)
    vb = jnp.concatenate([vp[:, w:w + nc] for w in range(ATT_BAND)], axis=2)
    s = jnp.einsum('bcqhd,bckhd->bchqk', q, kb).astype(jnp.float32) * (ATT_HEAD_DIM ** -0.5)
    kpos = np.arange(ATT_BAND * CHUNK) - ATT_LEFT_CHUNKS * CHUNK
    rel = np.clip(kpos[None, :] - np.arange(CHUNK)[:, None], -REL_CLIP, REL_CLIP) + REL_CLIP
    bias = rel_bias.astype(jnp.float32)[:, rel]
    kchunk = np.arange(nc)[:, None] + np.arange(ATT_BAND * CHUNK)[None, :] // CHUNK - ATT_LEFT_CHUNKS
    valid = jnp.asarray(kchunk >= 0)
    s = jnp.where(valid[None, :, None, None, :], s + bias[None, None], -jnp.inf)
    p = jax.nn.softmax(s, axis=-1)
    o = jnp.einsum('bchqk,bckhd->bcqhd', p.astype(v.dtype), vb)
    return o.reshape(b, l, ATT_WIDTH)


def _s5(u, lam_re, lam_im, log_dt, b_re, b_im, c_re, c_im, d_skip, w_glu, b_glu):
    bsz, l, _ = u.shape
    f32 = jnp.float32
    uf = u.astype(f32).reshape(bsz, l, SSM_GROUPS, SSM_GROUP)
    lr, li = lam_re.astype(f32), lam_im.astype(f32)
    dt = jnp.exp(log_dt.astype(f32))[:, None]
    mag = jnp.exp(lr * dt)
    ar, ai = mag * jnp.cos(li * dt), mag * jnp.sin(li * dt)
    den = lr * lr + li * li
    nr, ni = ar - 1.0, ai
    zr, zi = (nr * lr + ni * li) / den, (ni * lr - nr * li) / den
    br, bi = b_re.astype(f32), b_im.astype(f32)
    bbr = zr[..., None] * br - zi[..., None] * bi
    bbi = zr[..., None] * bi + zi[..., None] * br
    bu_r = jnp.einsum('blgn,gpn->blgp', uf, bbr)
    bu_i = jnp.einsum('blgn,gpn->blgp', uf, bbi)
    a_r = jnp.broadcast_to(ar, bu_r.shape)
    a_i = jnp.broadcast_to(ai, bu_i.shape)

    def combine(e1, e2):
        a1r, a1i, b1r, b1i = e1
        a2r, a2i, b2r, b2i = e2
        return (a2r * a1r - a2i * a1i, a2r * a1i + a2i * a1r,
                a2r * b1r - a2i * b1i + b2r, a2r * b1i + a2i * b1r + b2i)

    _, _, xr, xi = lax.associative_scan(combine, (a_r, a_i, bu_r, bu_i), axis=1)
    y = (jnp.einsum('blgp,gnp->blgn', xr, c_re.astype(f32))
         - jnp.einsum('blgp,gnp->blgn', xi, c_im.astype(f32)))
    y = y.reshape(bsz, l, SSM_WIDTH) + d_skip.astype(f32) * u.astype(f32)
    y = jax.nn.gelu(y)
    y = y * jax.nn.sigmoid(y @ w_glu.astype(f32) + b_glu.astype(f32))
    return y.astype(u.dtype)


def _causal_conv(x, w, b):
    k = w.shape[0]
    y = lax.conv_general_dilated(x, w[:, None, :], window_strides=(1,), padding=[(k - 1, 0)],
                                 dimension_numbers=('NWC', 'WIO', 'NWC'),
                                 feature_group_count=x.shape[-1])
    return y + b


def _mlstm(x_m, v_m, o_m, i_pre, f_pre, conv_w, conv_b, wq, wk, b_i, b_f, g_h, skip):
    bsz, l, _ = x_m.shape
    nc = l // CHUNK
    f32 = jnp.float32
    H, Dh = MLSTM_HEADS, MLSTM_HEAD_DIM
    xc = jax.nn.silu(_causal_conv(x_m, conv_w, conv_b))
    xh = xc.reshape(bsz, l, H, Dh)
    q = jnp.einsum('blhd,hde->blhe', xh, wq)
    k = jnp.einsum('blhd,hde->blhe', xh, wk) * (Dh ** -0.5)
    v = v_m.reshape(bsz, l, H, Dh)
    ig = (i_pre + b_i).astype(f32)
    lf = jax.nn.log_sigmoid((f_pre + b_f).astype(f32))

    def chunks(t):
        return t.astype(f32).reshape(bsz, nc, CHUNK, H, -1).transpose(1, 0, 3, 2, 4)

    qc, kc, vc = chunks(q), chunks(k), chunks(v)
    ic = ig.reshape(bsz, nc, CHUNK, H).transpose(1, 0, 3, 2)
    fc = lf.reshape(bsz, nc, CHUNK, H).transpose(1, 0, 3, 2)
    tri = jnp.tril(jnp.ones((CHUNK, CHUNK), dtype=bool))

    def step(carry, inp):
        cmat, nvec, m = carry
        qq, kk, vv, ii, ff = inp
        bcum = jnp.cumsum(ff, axis=-1)
        dmat = jnp.where(tri, bcum[..., :, None] - bcum[..., None, :] + ii[..., None, :], -jnp.inf)
        inter = bcum + m[..., None]
        m_row = jnp.maximum(jnp.max(dmat, axis=-1), inter)
        w_intra = jnp.exp(dmat - m_row[..., None])
        w_inter = jnp.exp(inter - m_row)
        s = jnp.einsum('bhtd,bhsd->bhts', qq, kk) * w_intra
        num = (jnp.einsum('bhts,bhsd->bhtd', s, vv)
               + w_inter[..., None] * jnp.einsum('bhvk,bhtk->bhtv', cmat, qq))
        den = jnp.sum(s, axis=-1) + w_inter * jnp.einsum('bhtk,bhk->bht', qq, nvec)
        h = num / jnp.maximum(jnp.abs(den), jnp.exp(-m_row))[..., None]
        b_last = bcum[..., -1]
        g = b_last[..., None] - bcum + ii
        m_new = jnp.maximum(b_last + m, jnp.max(g, axis=-1))
        wg = jnp.exp(g - m_new[..., None])
        decay = jnp.exp(b_last + m - m_new)
        cmat = decay[..., None, None] * cmat + jnp.einsum('bhtv,bhtk->bhvk', vv * wg[..., None], kk)
        nvec = decay[..., None] * nvec + jnp.einsum('bht,bhtk->bhk', wg, kk)
        return (cmat, nvec, m_new), h

    init = (jnp.zeros((bsz, H, Dh, Dh), f32), jnp.zeros((bsz, H, Dh), f32),
            jnp.full((bsz, H), MLSTM_NEG, f32))
    _, hs = lax.scan(step, init, (qc, kc, vc, ic, fc))
    hs = hs.transpose(1, 0, 3, 2, 4).reshape(bsz, l, H, Dh)
    hn = _rms(hs, g_h.reshape(H, Dh)).reshape(bsz, l, MLSTM_WIDTH)
    hn = hn + skip.astype(f32) * xc.astype(f32)
    return (jax.nn.sigmoid(o_m.astype(f32)) * hn).astype(x_m.dtype)


def _mem_attention(h, mem_n, w_q, w_kv, g_q, g_k, w_o):
    b, l, _ = h.shape
    m = mem_n.shape[1]
    q = _rms((h @ w_q).reshape(b, l, MEM_HEADS, MEM_HEAD_DIM), g_q)
    kv = (mem_n @ w_kv).reshape(b, m, 2, MEM_HEADS, MEM_HEAD_DIM)
    k = _rms(kv[:, :, 0], g_k)
    v = kv[:, :, 1]
    s = jnp.einsum('blhd,bmhd->bhlm', q, k).astype(jnp.float32) * (MEM_HEAD_DIM ** -0.5)
    p = jax.nn.softmax(s, axis=-1)
    o = jnp.einsum('bhlm,bmhd->blhd', p.astype(v.dtype), v).reshape(b, l, D_MODEL)
    return o @ w_o


def _swiglu(h, w_gu, w_down):
    g, u = jnp.split(h @ w_gu, 2, axis=-1)
    return (jax.nn.silu(g) * u) @ w_down


def setup_inputs(seed: int = 0) -> dict:
    key = jax.random.key(seed)
    ks = iter(jax.random.split(key, 64))
    f32 = jnp.float32

    def nrm(shape, scale):
        return jax.random.normal(next(ks), shape, f32) * scale

    def gain(shape):
        return 1.0 + nrm(shape, 0.02)

    L, D = DEPTH, D_MODEL
    G, P, N = SSM_GROUPS, SSM_STATE, SSM_GROUP
    H, Dh = MLSTM_HEADS, MLSTM_HEAD_DIM
    log_dt = jax.random.uniform(next(ks), (L, G), f32, math.log(SSM_DT_MIN), math.log(SSM_DT_MAX))
    lam_im = jnp.pi * jnp.arange(P, dtype=f32)[None, None, :] + nrm((L, G, P), 0.01)
    b_f = jnp.linspace(3.0, 6.0, H, dtype=f32)[None, :] + nrm((L, H), 0.01)
    return {
        "x": nrm((BATCH, SEQ, D), 1.0),
        "mem": nrm((BATCH, MEM_TOKENS, D), 1.0),
        "g_mem": gain((D,)),
        "norm_mix": gain((L, D)),
        "w_in": nrm((L, D, IN_WIDTH), D ** -0.5),
        "b_gate": nrm((L, N_BRANCH * D), 0.01),
        "g_qa": gain((L, ATT_HEAD_DIM)),
        "g_ka": gain((L, ATT_HEAD_DIM)),
        "rel_bias": nrm((L, ATT_HEADS, 2 * REL_CLIP + 1), 0.1),
        "lam_re": -0.5 + nrm((L, G, P), 0.01),
        "lam_im": lam_im,
        "log_dt": log_dt,
        "b_re": nrm((L, G, P, N), (2 * N) ** -0.5),
        "b_im": nrm((L, G, P, N), (2 * N) ** -0.5),
        "c_re": nrm((L, G, N, P), (2 * P) ** -0.5),
        "c_im": nrm((L, G, N, P), (2 * P) ** -0.5),
        "d_skip": nrm((L, SSM_WIDTH), 0.5),
        "w_glu": nrm((L, SSM_WIDTH, SSM_WIDTH), SSM_WIDTH ** -0.5),
        "b_glu": nrm((L, SSM_WIDTH), 0.01),
        "conv_w": nrm((L, MLSTM_CONV, MLSTM_WIDTH), MLSTM_CONV ** -0.5),
        "conv_b": nrm((L, MLSTM_WIDTH), 0.01),
        "wq_m": nrm((L, H, Dh, Dh), Dh ** -0.5),
        "wk_m": nrm((L, H, Dh, Dh), Dh ** -0.5),
        "b_i": nrm((L, H), 0.1),
        "b_f": b_f,
        "g_hm": gain((L, MLSTM_WIDTH)),
        "skip_m": gain((L, MLSTM_WIDTH)),
        "w_br_a": nrm((L, ATT_WIDTH, D), ATT_WIDTH ** -0.5),
        "w_br_s": nrm((L, SSM_WIDTH, D), SSM_WIDTH ** -0.5),
        "w_br_m": nrm((L, MLSTM_WIDTH, D), MLSTM_WIDTH ** -0.5),
        "w_out": nrm((L, D, D), D ** -0.5),
        "norm_x": gain((L, D)),
        "w_xq": nrm((L, D, D), D ** -0.5),
        "w_xkv": nrm((L, D, 2 * D), D ** -0.5),
        "g_xq": gain((L, MEM_HEAD_DIM)),
        "g_xk": gain((L, MEM_HEAD_DIM)),
        "w_xo": nrm((L, D, D), D ** -0.5),
        "norm_ffn": gain((L, D)),
        "w_gu": nrm((L, D, 2 * FFN_HIDDEN), D ** -0.5),
        "w_down": nrm((L, FFN_HIDDEN, D), FFN_HIDDEN ** -0.5),
    }


def reference(x, mem, g_mem, norm_mix, w_in, b_gate, g_qa, g_ka, rel_bias, lam_re, lam_im,
              log_dt, b_re, b_im, c_re, c_im, d_skip, w_glu, b_glu, conv_w, conv_b, wq_m,
              wk_m, b_i, b_f, g_hm, skip_m, w_br_a, w_br_s, w_br_m, w_out, norm_x, w_xq,
              w_xkv, g_xq, g_xk, w_xo, norm_ffn, w_gu, w_down):
    b, l, _ = x.shape
    mem_n = _rms(mem, g_mem)
    splits = _in_split_points()
    for i in range(DEPTH):
        h = _rms(x, norm_mix[i])
        qa, ka, va, us, xm, vm, om, im, fm, gpre = jnp.split(h @ w_in[i], splits, axis=-1)
        ya = _chunk_attention(qa, ka, va, g_qa[i], g_ka[i], rel_bias[i])
        ys = _s5(us, lam_re[i], lam_im[i], log_dt[i], b_re[i], b_im[i], c_re[i], c_im[i],
                 d_skip[i], w_glu[i], b_glu[i])
        ym = _mlstm(xm, vm, om, im, fm, conv_w[i], conv_b[i], wq_m[i], wk_m[i], b_i[i], b_f[i],
                    g_hm[i], skip_m[i])
        gates = jax.nn.sigmoid(gpre + b_gate[i]).reshape(b, l, N_BRANCH, D_MODEL)
        merged = (gates[:, :, 0] * (ya @ w_br_a[i])
                  + gates[:, :, 1] * (ys @ w_br_s[i])
                  + gates[:, :, 2] * (ym @ w_br_m[i]))
        x = x + merged @ w_out[i]
        x = x + _mem_attention(_rms(x, norm_x[i]), mem_n, w_xq[i], w_xkv[i], g_xq[i], g_xk[i], w_xo[i])
        x = x + _swiglu(_rms(x, norm_ffn[i]), w_gu[i], w_down[i])
    return x
```

```python
import contextlib
import math
import numpy as np
import concourse.bass as bass
import concourse.mybir as mybir
from concourse.bass_utils import run_bass_kernel_spmd

F32 = mybir.dt.float32
BF16 = mybir.dt.bfloat16
I32 = mybir.dt.int32
ALU = mybir.AluOpType
AF = mybir.ActivationFunctionType
AX = mybir.AxisListType

COMPUTE = ("tensor", "vector", "scalar", "gpsimd")
ALLENG = ("tensor", "vector", "scalar", "gpsimd", "sync")

T = 2048
D = 2048
NT = 16
DC = 16
INW = 13064
FF = 5632
EPS = 1e-6
DEPTH = 2
C_QA, C_KA, C_VA, C_US, C_XM, C_VM, C_OM, C_I, C_F, C_G = 0, 1024, 2048, 3072, 3840, 4864, 5888, 6912, 6916, 6920
Y_A, Y_S, Y_M = 0, 1024, 1792
PV_NMIX, PV_NX, PV_NF, PV_BG, PV_GQA, PV_GKA, PV_DSK, PV_BGLU, PV_CW, PV_CB, PV_GH, PV_SK, PV_GXQ, PV_GXK, PV_GMEM = (
    0, 16, 32, 48, 96, 97, 98, 104, 110, 142, 150, 158, 166, 170, 174)
NPV = 190


class Buf:
    __slots__ = ("name", "w", "re", "rd")

    def __init__(self, name=""):
        self.name = name
        self.w = None
        self.re = {}
        self.rd = []


class Op:
    __slots__ = ("eng", "fn", "waits", "inc", "cnt", "dma", "dsem", "dval")


class Prog:
    def __init__(self, nc, stack, n_dma_sems=(("sync", 24), ("gpsimd", 16)), same_eng_sync=True):
        self.nc = nc
        self.same_eng_sync = same_eng_sync
        self.ops = {e: [] for e in ALLENG}
        self.cnt = {e: 0 for e in COMPUTE}
        self.pending = {e: [] for e in COMPUTE}
        self.waited = {e: {} for e in ALLENG}
        self.esem = {e: stack.enter_context(nc.semaphore("es_" + e)) for e in COMPUTE}
        self.dsems = {}
        self.dnext = {}
        for e, n in n_dma_sems:
            self.dsems[e] = [[stack.enter_context(nc.semaphore("ds_%s%d" % (e, i))), 0, None] for i in range(n)]
            self.dnext[e] = 0
        self.nops = 0

    def _wait_for(self, op, d):
        if d is None or d is op:
            return
        if d.dma:
            key, val = d.dsem, d.dval
        else:
            if d.eng == op.eng and not op.dma:
                if d.eng == "tensor" or not self.same_eng_sync:
                    return
            if d.cnt is None:
                raise RuntimeError("dependency on op without milestone inc (%s)" % d.eng)
            key, val = self.esem[d.eng], d.cnt
        w = self.waited[op.eng]
        k = id(key)
        if w.get(k, 0) >= val:
            return
        w[k] = val
        for i, (s, v) in enumerate(op.waits):
            if s is key:
                op.waits[i] = (s, max(v, val))
                return
        op.waits.append((key, val))

    def op(self, eng, fn, reads=(), writes=(), inc=True, dma=False):
        o = Op()
        o.eng, o.fn, o.waits, o.inc, o.dma, o.cnt, o.dsem, o.dval = eng, fn, [], inc, dma, None, None, 0
        self.nops += 1
        for b in reads:
            self._wait_for(o, b.w)
        for b in writes:
            self._wait_for(o, b.w)
            for r in b.re.values():
                self._wait_for(o, r)
            for r in b.rd:
                self._wait_for(o, r)
        if dma:
            slots = self.dsems[eng]
            i = self.dnext[eng]
            self.dnext[eng] = (i + 1) % len(slots)
            slot = slots[i]
            if slot[2] is not None:
                self._wait_for(o, slot[2])
            slot[1] += 16
            slot[2] = o
            o.dsem, o.dval = slot[0], slot[1]
        elif inc:
            self.cnt[eng] += 1
            o.cnt = self.cnt[eng]
            for p in self.pending[eng]:
                p.cnt = o.cnt
            self.pending[eng] = []
        else:
            self.pending[eng].append(o)
        for b in reads:
            if dma:
                b.rd.append(o)
            else:
                b.re[eng] = o
        for b in writes:
            b.w = o
            b.re = {}
            b.rd = []
        self.ops[eng].append(o)
        return o

    def dma(self, eng, out, in_, reads=(), writes=(), **kw):
        return self.op(eng, I("dma_start", out=out, in_=in_, **kw), reads, writes, dma=True)

    def final_wait(self, eng="sync"):
        o = Op()
        o.eng, o.fn, o.waits, o.inc, o.dma, o.cnt, o.dsem, o.dval = eng, None, [], False, False, None, None, 0
        for e, slots in self.dsems.items():
            for s in slots:
                if s[2] is not None:
                    self._wait_for(o, s[2])
        for e in COMPUTE:
            for d in reversed(self.ops[e]):
                if not d.dma and d.cnt is not None:
                    self._wait_for(o, d)
                    break
        self.ops[eng].append(o)

    def emit(self):
        nc = self.nc
        with nc.Block() as block:
            for ename in ALLENG:
                ops = self.ops[ename]
                if not ops:
                    continue
                esem = self.esem.get(ename)

                def body(eng, ops=ops, esem=esem):
                    for o in ops:
                        for (s, v) in o.waits:
                            eng.wait_ge(s, v)
                        if o.fn is None:
                            continue
                        ins = o.fn(eng)
                        if o.dma:
                            ins.then_inc(o.dsem, 16)
                        elif o.inc:
                            ins.then_inc(esem, 1)

                getattr(block, ename)(body)


class Arena:
    def __init__(self, nc, stack, nbytes):
        self.cap = nbytes
        self.t = stack.enter_context(nc.sbuf_tensor("arena", [128, nbytes // 4], F32))
        self.top = 0
        self.regions = []

    def alloc(self, shape, dtype, nbufs=1):
        esz = 2 if dtype == BF16 else 4
        free = 1
        for s in shape[1:]:
            free *= s
        nb = (free * esz + 63) // 64 * 64
        start = self.top
        self.top += nb
        if self.top > self.cap:
            raise RuntimeError("arena overflow: %d > %d" % (self.top, self.cap))
        v = self.t[0:shape[0], start // 4:(start + nb) // 4]
        if dtype != F32:
            v = v.bitcast(dtype)
        v = v[:, 0:free]
        if len(shape) == 3:
            v = v.rearrange("p (a b) -> p a b", a=shape[1])
        elif len(shape) == 4:
            v = v.rearrange("p (a b c) -> p a b c", a=shape[1], b=shape[2])
        bufs = [Buf() for _ in range(nbufs)]
        keep = []
        for (s, e, ob) in self.regions:
            if s < start + nb and start < e:
                for o in ob:
                    for n in bufs:
                        if o.w is not None:
                            if o.w.dma:
                                n.rd.append(o.w)
                            else:
                                n.re[o.w.eng] = _later(n.re.get(o.w.eng), o.w)
                        for en, r in o.re.items():
                            n.re[en] = _later(n.re.get(en), r)
                        n.rd.extend(o.rd)
                if s >= start and e <= start + nb:
                    continue
            keep.append((s, e, ob))
        keep.append((start, start + nb, bufs))
        self.regions = keep
        return (v, bufs[0]) if nbufs == 1 else (v, bufs)

    def mark(self):
        return self.top

    def release(self, m):
        self.top = m


def _later(a, b):
    if a is None:
        return b
    if a.cnt is None:
        return a if b.cnt is not None else b
    if b.cnt is None:
        return b
    return a if a.cnt >= b.cnt else b


class Ctx:
    pass


def I(method, *a, **kw):
    return lambda e: getattr(e, method)(*a, **kw)


def mm(ps, lhsT, rhs, start, stop):
    return I("matmul", ps, lhsT=lhsT, rhs=rhs, start=start, stop=stop)


def build_nc(stages=("all",), debug=()):
    nc = bass.Bass("TRN2", target_bir_lowering=False)
    K = Ctx()
    K.nc = nc
    K.debug = debug

    def din(name, shape, dt=F32):
        return nc.dram_tensor(name, list(shape), dt, kind="ExternalInput")

    def dscr(name, shape, dt=F32):
        kind = "ExternalOutput" if name in debug else None
        if kind:
            return nc.dram_tensor(name, list(shape), dt, kind=kind)
        return nc.dram_tensor(name, list(shape), dt)

    K.x_in = din("x", [T, D])
    K.mem_in = din("mem", [256, D])
    K.w_in = din("w_in", [DEPTH, D, INW])
    K.rel_bias = din("rel_bias", [DEPTH, 8, 513])
    K.w_glu = din("w_glu", [DEPTH, 768, 768])
    K.wq_m = din("wq_m", [DEPTH, 4, 256, 256])
    K.wk_m = din("wk_m", [DEPTH, 4, 256, 256])
    K.w_br_a = din("w_br_a", [DEPTH, 1024, D])
    K.w_br_s = din("w_br_s", [DEPTH, 768, D])
    K.w_br_m = din("w_br_m", [DEPTH, 1024, D])
    K.w_out = din("w_out", [DEPTH, D, D])
    K.w_xq = din("w_xq", [DEPTH, D, D])
    K.w_xkv = din("w_xkv", [DEPTH, D, 2 * D])
    K.w_xo = din("w_xo", [DEPTH, D, D])
    K.w_gu = din("w_gu", [DEPTH, D, 2 * FF])
    K.w_down = din("w_down", [DEPTH, FF, D])
    K.pvec = din("pvec", [DEPTH, 128, NPV])
    K.pgate = din("pgate", [DEPTH, 4, 2])
    K.s5p = din("s5p", [DEPTH, 128, 24 * 3])
    K.s5b = din("s5b", [DEPTH, 2, 128, 24 * 16])
    K.s5c = din("s5c", [DEPTH, 2, 128, 24 * 16])
    K.consts = din("consts", [128, NCONST])
    K.out = nc.dram_tensor("out", [T, D], F32, kind="ExternalOutput")

    K.xs = dscr("xs", [T, D])
    K.qaT = dscr("qaT", [1024, T], BF16)
    K.kaT = dscr("kaT", [1024, T], BF16)
    K.va = dscr("va", [T, 1024], BF16)
    K.usT = dscr("usT", [768, T])
    K.xmT = dscr("xmT", [1024, T])
    K.vm = dscr("vm", [T, 1024])
    K.omT = dscr("omT", [1024, T], BF16)
    K.ifT = dscr("ifT", [8, T])
    K.gT = dscr("gT", [6144, T], BF16)
    K.yT = dscr("yT", [2816, T], BF16)
    K.ext = dscr("ext", [8, 1024])
    K.db = {}

    with contextlib.ExitStack() as st:
        P = Prog(nc, st)
        K.P = P
        K.A = Arena(nc, st, 206 * 1024)
        K.ps = []
        for i in range(8):
            t = st.enter_context(nc.psum_tensor("ps%d" % i, [128, 512], F32))
            K.ps.append((t[:, :], Buf("ps%d" % i)))
        K.psn = 0
        setup_consts(K)
        for li in range(DEPTH):
            if "all" in stages or ("s1", li) in stages:
                stage_inproj(K, li)
            if "all" in stages or ("att", li) in stages:
                stage_attn(K, li)
            if "all" in stages or ("s5", li) in stages:
                stage_s5(K, li)
            if "all" in stages or ("ml", li) in stages:
                stage_mlstm(K, li)
            if "all" in stages or ("mrg", li) in stages:
                stage_merge(K, li)
            if "all" in stages or ("xat", li) in stages:
                stage_xattn(K, li)
            if "all" in stages or ("ffn", li) in stages:
                stage_ffn(K, li)
        P.final_wait("sync")
        P.emit()
    return nc


def psum(K):
    i = K.psn
    K.psn = (i + 1) % 8
    return K.ps[i]


def dbuf(K, key):
    b = K.db.get(key)
    if b is None:
        b = Buf(str(key))
        K.db[key] = b
    return b


NCONST = 544


def host_consts():
    c = np.zeros((128, NCONST), np.float32)
    c[:, 0:128] = np.eye(128)
    c[:, 128:256] = np.eye(128)[::-1]
    c[:, 256:384] = 1.0
    c[:, 384:512] = np.triu(np.ones((128, 128)))
    for jm in range(4):
        for q in range(128):
            c[q, 512 + jm * 8 + 2 * jm + (1 if q >= 64 else 0)] = 1.0
    return c


def setup_consts(K):
    P, A = K.P, K.A
    K.cst, K.cstb = A.alloc([128, NCONST], F32)
    P.dma("sync", K.cst, K.consts.ap()[:, :], writes=[K.cstb])
    K.ident = K.cst[:, 0:128]
    K.antiI = K.cst[:, 128:256]
    K.ones = K.cst[:, 256:384]
    K.tri = K.cst[:, 384:512]
    K.gmask = K.cst[:, 512:544]
    K.identb, K.cbb = A.alloc([128, 256], BF16)
    P.op("vector", I("tensor_copy", out=K.identb[:, 0:128], in_=K.ident), [K.cstb], [K.cbb])
    P.op("vector", I("tensor_copy", out=K.identb[:, 128:256], in_=K.ones), [K.cstb], [K.cbb])
    K.onesb = K.identb[:, 128:256]
    K.identbb = K.identb[:, 0:128]
    K.pv, K.pvb = A.alloc([128, DEPTH, NPV], F32)
    for li in range(DEPTH):
        P.dma("sync", K.pv[:, li, :], K.pvec.ap()[li], writes=[K.pvb])
    K.pvd, K.pvdb = A.alloc([128, DEPTH, 8], F32)
    for li in range(DEPTH):
        P.op("vector", I("tensor_scalar_mul", out=K.pvd[:, li, 0:1], in0=K.pv[:, li, PV_GQA:PV_GQA + 1],
                                                              scalar1=128 ** -0.5), [K.pvb], [K.pvdb])
        P.op("vector", I("tensor_scalar_mul", out=K.pvd[:, li, 1:5], in0=K.pv[:, li, PV_GXK:PV_GXK + 4],
                                                              scalar1=512 ** -0.5), [K.pvb], [K.pvdb])
    K.memT, mtb = A.alloc([128, DC, 256], BF16, nbufs=2)
    K.memTb = mtb

    def msrc(tt):
        return K.mem_in.ap()[tt * 128:(tt + 1) * 128, :], []

    norm_to_T(K, 0, msrc, 2, PV_GMEM, K.memT, mtb)
    K.base_mark = A.mark()


def xkeys(K, tt, c0=0, c1=D):
    return [dbuf(K, ("xs", tt, q)) for q in range(c0 // 256, (c1 + 255) // 256)]


def pvcol(K, li, col, n=1):
    return K.pv[:, li, col:col + n]


def norm_to_T(K, li, src_fn, ntiles, gcol, hT, hbufs, pvl=None):
    P, A = K.P, K.A
    m = A.mark()
    xts = [A.alloc([128, D], F32) for _ in range(2)]
    hbs = [A.alloc([128, D], BF16) for _ in range(2)]
    junk, jb = A.alloc([128, D], F32)
    st, stb = A.alloc([128, 4], F32, nbufs=2)
    pl = li if pvl is None else pvl
    for tt in range(ntiles):
        src, sbuf = src_fn(tt)
        xt, xb = xts[tt % 2]
        hb, hbb = hbs[tt % 2]
        ss = st[:, (tt % 2) * 2:(tt % 2) * 2 + 1]
        rs = st[:, (tt % 2) * 2 + 1:(tt % 2) * 2 + 2]
        sb = stb[tt % 2]
        P.dma("sync", xt, src, reads=sbuf, writes=[xb])
        P.op("gpsimd", I("memset", ss, 0.0), [], [sb])
        P.op("scalar", I("activation", out=junk, in_=xt, func=AF.Square, accum_out=ss),
             [xb, sb], [jb, sb])
        P.op("scalar", I("activation", out=rs, in_=ss, func=AF.Sqrt, scale=1.0 / D, bias=EPS),
             [sb], [sb])
        P.op("vector", I("reciprocal", out=rs, in_=rs), [sb], [sb])
        P.op("vector", I("tensor_scalar", out=hb, in0=xt, scalar1=rs, scalar2=None,
                                                                       op0=ALU.mult), [xb, sb], [hbb])
        for half in range(2):
            pt, pb = psum(K)
            ptb = pt[:, :].bitcast(BF16).rearrange("p (a b) -> p a b", a=8)
            for cc in range(8):
                c = half * 8 + cc
                P.op("tensor", I("transpose", out=ptb[:, cc, :],
                                                                                 in_=hb[:, c * 128:(c + 1) * 128],
                                                                                 identity=K.identbb),
                     [hbb, K.cbb], [pb], inc=(cc == 7))
            g = K.pv[:, pl, gcol + half * 8:gcol + half * 8 + 8].unsqueeze(2).to_broadcast([128, 8, 128])
            P.op("vector", I("tensor_tensor",
                out=hT[:, half * 8:half * 8 + 8, tt * 128:(tt + 1) * 128], in0=ptb, in1=g, op=ALU.mult),
                [pb, K.pvb], [hbufs[tt]])
    A.release(m)


class WStream:
    def __init__(self, K, nslots, nelem):
        self.K = K
        self.slots = [K.A.alloc([128, nelem], BF16) for _ in range(nslots)]
        self.i = 0
        self.nelem = nelem

    def load(self, src3d, nch, ncols):
        v, b = self.slots[self.i]
        self.i = (self.i + 1) % len(self.slots)
        assert nch * ncols <= self.nelem
        dst = v[:, 0:nch * ncols].rearrange("p (a b) -> p a b", a=nch)
        self.K.P.dma("gpsimd", dst, src3d, writes=[b])
        return dst, b


def wsrc(h, li, r0, nch, c0, ncols):
    return h.ap()[li, r0:r0 + nch * 128, c0:c0 + ncols].rearrange("(c p) n -> p c n", p=128)


class Stager:
    def __init__(self, K, n, shape, dtype):
        self.t = [K.A.alloc(shape, dtype) for _ in range(n)]
        self.i = 0

    def next(self):
        r = self.t[self.i]
        self.i = (self.i + 1) % len(self.t)
        return r


def stage_inproj(K, li):
    P, A, nc = K.P, K.A, K.nc
    m0 = A.mark()
    hT, hbufs = A.alloc([128, DC, T], BF16, nbufs=NT)
    xsrc = K.x_in if li == 0 else K.xs

    def src_fn(tt):
        return xsrc.ap()[tt * 128:(tt + 1) * 128, :], xkeys(K, tt)

    norm_to_T(K, li, src_fn, NT, PV_NMIX, hT, hbufs)
    if "dbg_hT" in K.debug:
        dh = K.nc.dram_tensor("dbg_hT", [128, DC * T], BF16, kind="ExternalOutput")
        P.dma("sync", dh.ap()[:, :], hT.rearrange("p a b -> p (a b)"), reads=hbufs)
        dp = K.nc.dram_tensor("dbg_pv", [128, DEPTH * NPV], F32, kind="ExternalOutput")
        P.dma("sync", dp.ap()[:, :], K.pv.rearrange("p a b -> p (a b)"), reads=[K.pvb])
        return
    W = WStream(K, 3, 16 * 512)
    stg32 = Stager(K, 2, [128, T], F32)
    stg16 = Stager(K, 2, [128, T], BF16)
    sq_t = [A.alloc([128, 512], F32) for _ in range(2)]
    rt_t = [A.alloc([128, 512], F32) for _ in range(2)]
    cnt = [0]

    def fm_block(c0, ncols, evac, post):
        wv, wb = W.load(wsrc(K.w_in, li, 0, DC, c0, ncols), DC, ncols)
        nj = (ncols + 127) // 128
        for j in range(nj):
            mcols = min(128, ncols - j * 128)
            for tb in range(4):
                ps, pb = psum(K)
                for c in range(DC):
                    P.op("tensor", mm(ps[0:mcols, :], wv[:, c, j * 128:j * 128 + mcols], hT[:, c, tb * 512:(tb + 1) * 512],
                                      c == 0, c == DC - 1), [wb] + hbufs[tb * 4:tb * 4 + 4], [pb], inc=(c == DC - 1))
                evac(j, tb, ps, pb, mcols)
            post(j)

    for which, c_base, dst, gsrc in (("q", C_QA, K.qaT, lambda: K.pvd[:, li, 0:1]),
                                     ("k", C_KA, K.kaT, lambda: pvcol(K, li, PV_GKA))):
        for blk in range(2):
            cur = {}

            def evac(j, tb, ps, pb, mcols, cur=cur, gsrc=gsrc):
                if tb == 0:
                    cur["s"] = stg16.next()
                sv, sb = cur["s"]
                k = cnt[0]
                cnt[0] += 1
                sq, sqb = sq_t[k % 2]
                rt, rtb = rt_t[k % 2]
                P.op("scalar", I("activation", out=sq, in_=ps, func=AF.Square), [pb], [sqb])
                ps2, pb2 = psum(K)
                P.op("tensor", mm(ps2, K.ones, sq, True, True), [K.cstb, sqb], [pb2])
                P.op("scalar", I("activation", out=rt, in_=ps2, func=AF.Sqrt, scale=1.0 / 128, bias=EPS),
                     [pb2], [rtb])
                P.op("vector", I("reciprocal", out=rt, in_=rt), [rtb], [rtb])
                g = gsrc()
                P.op("vector", I("scalar_tensor_tensor", out=sv[:, tb * 512:(tb + 1) * 512], in0=ps, scalar=g,
                                                                in1=rt, op0=ALU.mult, op1=ALU.mult),
                     [pb, rtb, K.pvb, K.pvdb], [sb])

            def post(j, cur=cur, dst=dst, blk=blk):
                sv, sb = cur["s"]
                hrow = (blk * 4 + j) * 128
                P.dma("sync", dst.ap()[hrow:hrow + 128, :], sv, reads=[sb], writes=[dbuf(K, (dst.name, blk * 4 + j))])

            fm_block(c_base + blk * 512, 512, evac, post)

    def fm_plain(c_base, ncols_total, dst, dt, kind, key):
        nblk = (ncols_total + 511) // 512
        for blk in range(nblk):
            nc_ = min(512, ncols_total - blk * 512)
            cur = {}

            def evac(j, tb, ps, pb, mcols, cur=cur, blk=blk):
                if tb == 0:
                    cur["s"] = (stg32 if dt == F32 else stg16).next()
                sv, sb = cur["s"]
                o = sv[0:mcols, tb * 512:(tb + 1) * 512]
                if kind == "copy":
                    if (tb % 2) == 0:
                        P.op("scalar", I("copy", out=o, in_=ps[0:mcols, :]), [pb], [sb])
                    else:
                        P.op("vector", I("tensor_copy", out=o, in_=ps[0:mcols, :]), [pb], [sb])
                elif kind == "sig":
                    P.op("scalar", I("activation", out=o, in_=ps[0:mcols, :], func=AF.Sigmoid), [pb], [sb])
                elif kind == "sigb":
                    bcol = pvcol(K, li, PV_BG + blk * 4 + j)
                    P.op("scalar", I("activation", out=o, in_=ps[0:mcols, :], func=AF.Sigmoid, bias=bcol),
                         [pb, K.pvb], [sb])

            def post(j, cur=cur, blk=blk, nc_=nc_):
                sv, sb = cur["s"]
                r0 = blk * 512 + j * 128
                mcols = min(128, nc_ - j * 128)
                P.dma("sync", dst.ap()[r0:r0 + mcols, :], sv[0:mcols, :], reads=[sb],
                      writes=[dbuf(K, (key, r0 // 128))])

            fm_block(c_base + blk * 512, nc_, evac, post)

    fm_plain(C_US, 768, K.usT, F32, "copy", "usT")
    fm_plain(C_XM, 1024, K.xmT, F32, "copy", "xmT")
    fm_plain(C_OM, 1024, K.omT, BF16, "sig", "omT")
    fm_plain(C_I, 8, K.ifT, F32, "copy", "ifT")
    fm_plain(C_G, 6144, K.gT, BF16, "sigb", "gT")

    stt32 = Stager(K, 3, [128, 512], F32)
    stt16 = Stager(K, 3, [128, 512], BF16)
    for c_base, dst, dt, key in ((C_VA, K.va, BF16, "va"), (C_VM, K.vm, F32, "vm")):
        for blk in range(2):
            wv, wb = W.load(wsrc(K.w_in, li, 0, DC, c_base + blk * 512, 512), DC, 512)
            for tt in range(NT):
                ps, pb = psum(K)
                for c in range(DC):
                    P.op("tensor", mm(ps, hT[:, c, tt * 128:(tt + 1) * 128], wv[:, c, :], c == 0, c == DC - 1),
                         [wb, hbufs[tt]], [pb], inc=(c == DC - 1))
                sv, sb = (stt32 if dt == F32 else stt16).next()
                if tt % 2 == 0:
                    P.op("scalar", I("copy", out=sv, in_=ps), [pb], [sb])
                else:
                    P.op("vector", I("tensor_copy", out=sv, in_=ps), [pb], [sb])
                P.dma("sync", dst.ap()[tt * 128:(tt + 1) * 128, blk * 512:(blk + 1) * 512], sv, reads=[sb],
                      writes=[dbuf(K, (key, tt, blk))])
    A.release(m0)


def stage_attn(K, li):
    P, A = K.P, K.A
    m0 = A.mark()
    E, Eb = A.alloc([8, 1024], F32)
    P.op("gpsimd", I("memset", E, 0.0), [], [Eb])
    P.dma("sync", E[:, 384:897], K.rel_bias.ap()[li], writes=[Eb])
    P.op("vector", I("tensor_copy", out=E[:, 0:384], in_=E[:, 384:385].to_broadcast([8, 384])), [Eb], [Eb])
    extb = dbuf(K, "ext")
    P.dma("sync", K.ext.ap()[:, :], E, reads=[Eb], writes=[extb])
    qs = [A.alloc([128, T], BF16) for _ in range(2)]
    ks = [A.alloc([128, T], BF16) for _ in range(2)]
    vs = [A.alloc([128, NT, 128], BF16) for _ in range(2)]
    Ls = [A.alloc([128, 5, 128], F32) for _ in range(2)]
    PTs = [A.alloc([128, 5, 128], BF16) for _ in range(2)]
    sts = [A.alloc([128, T], BF16) for _ in range(2)]
    rds = [A.alloc([128, 128], F32) for _ in range(2)]
    for (pt, ptb) in PTs:
        P.op("gpsimd", I("memset", pt, 0.0), [], [ptb])
    it = 0
    for h in range(8):
        qT, qb = qs[h % 2]
        kT, kb = ks[h % 2]
        V, vb = vs[h % 2]
        Lt, lb = Ls[h % 2]
        sv, sb = sts[h % 2]
        P.dma("sync", qT, K.qaT.ap()[h * 128:(h + 1) * 128, :], reads=[dbuf(K, ("qaT", h))], writes=[qb])
        P.dma("sync", kT, K.kaT.ap()[h * 128:(h + 1) * 128, :], reads=[dbuf(K, ("kaT", h))], writes=[kb])
        P.dma("sync", V, K.va.ap()[:, h * 128:(h + 1) * 128].rearrange("(t p) d -> p t d", p=128),
              reads=[dbuf(K, ("va", tt, h // 4)) for tt in range(NT)], writes=[vb])
        for m in range(5):
            P.dma("sync", Lt[:, m, :], bass.AP(K.ext, h * 1024 + 513 - 128 * m, [[1, 128], [1, 128]]),
                  reads=[extb], writes=[lb])
        for j in range(NT):
            PT, ptb = PTs[it % 2]
            rd, rdb = rds[it % 2]
            it += 1
            psA, pbA = psum(K)
            mmax = min(4, j)
            qsl = qT[:, j * 128:(j + 1) * 128]
            for m in range(min(3, mmax) + 1):
                kt = j - m
                tgt = psA[:, m * 128:(m + 1) * 128]
                P.op("tensor", mm(tgt, kT[:, kt * 128:(kt + 1) * 128], qsl, True, False), [kb, qb], [pbA], inc=False)
                P.op("tensor", mm(tgt, Lt[:, m, :], K.antiI, False, True), [lb, K.cstb], [pbA],
                     inc=(m == min(3, mmax)))
            if mmax == 4:
                psB, pbB = psum(K)
                kt = j - 4
                tgt = psB[:, 0:128]
                P.op("tensor", mm(tgt, kT[:, kt * 128:(kt + 1) * 128], qsl, True, False), [kb, qb], [pbB], inc=False)
                P.op("tensor", mm(tgt, Lt[:, 4, :], K.antiI, False, True), [lb, K.cstb], [pbB])
            P.op("scalar", I("activation", out=PT[0:64, 0, :], in_=psA[0:64, 0:128], func=AF.Exp),
                 [pbA], [ptb])
            P.op("scalar", I("activation", out=PT[64:128, 0, 64:128], in_=psA[64:128, 64:128],
                                                                   func=AF.Exp), [pbA], [ptb])
            n13 = min(3, mmax)
            if n13 >= 1:
                P.op("scalar", I("activation",
                    out=PT[:, 1:n13 + 1, :], in_=psA[:, 128:128 * (n13 + 1)].rearrange("p (a b) -> p a b", b=128),
                    func=AF.Exp), [pbA], [ptb])
            if mmax == 4:
                P.op("scalar", I("activation", out=PT[64:128, 4, :], in_=psB[64:128, 0:128],
                                                                       func=AF.Exp), [pbB], [ptb])
                P.op("scalar", I("activation", out=PT[0:64, 4, 0:64], in_=psB[0:64, 0:64],
                                                                       func=AF.Exp), [pbB], [ptb])
            psO, pbO = psum(K)
            for m in range(mmax + 1):
                kt = j - m
                P.op("tensor", mm(psO[:, 0:128], V[:, kt, :], PT[:, m, :], m == 0, m == mmax), [vb, ptb], [pbO], inc=False)
            for m in range(mmax + 1):
                P.op("tensor", mm(psO[:, 128:256], K.onesb, PT[:, m, :], m == 0, m == mmax), [K.cbb, ptb], [pbO],
                     inc=(m == mmax))
            P.op("vector", I("reciprocal", out=rd, in_=psO[:, 128:256]), [pbO], [rdb])
            P.op("vector", I("tensor_tensor",
                out=sv[:, j * 128:(j + 1) * 128], in0=psO[:, 0:128], in1=rd, op=ALU.mult), [pbO, rdb], [sb])
        P.dma("sync", K.yT.ap()[Y_A + h * 128:Y_A + (h + 1) * 128, :], sv, reads=[sb],
              writes=[dbuf(K, ("yT", (Y_A // 128) + h))])
    A.release(m0)


def stage_s5(K, li):
    P, A = K.P, K.A
    m0 = A.mark()
    V = "vector"
    sp, spb = A.alloc([128, 72], F32)
    P.dma("sync", sp, K.s5p.ap()[li], writes=[spb])
    bb, bbb = A.alloc([128, 2, 384], F32)
    cc, ccb = A.alloc([128, 2, 384], F32)
    for k in range(2):
        P.dma("sync", bb[:, k, :], K.s5b.ap()[li, k], writes=[bbb])
        P.dma("sync", cc[:, k, :], K.s5c.ap()[li, k], writes=[ccb])
    lr, lim, ldt = sp[:, 0:24], sp[:, 24:48], sp[:, 48:72]
    wk, wkb = A.alloc([128, 16, 24], F32)
    wi, wib = A.alloc([128, 24], I32)
    PR, prb = A.alloc([128, 11, 24], F32)
    PI, pib = A.alloc([128, 11, 24], F32)
    NPI, npb = A.alloc([128, 11, 24], F32)
    dt, mag, th, t1, kf, r, sn, cs, den, nr, tmpa, tmpb, zr, zi = [wk[:, i, :] for i in range(14)]
    TWO_PI = 2 * math.pi

    def vop(fn, reads, writes):
        P.op(V, fn, reads, writes)

    P.op("scalar", I("activation", out=dt, in_=ldt, func=AF.Exp), [spb], [wkb])
    vop(I("tensor_tensor", out=mag, in0=lr, in1=dt, op=ALU.mult), [spb, wkb], [wkb])
    P.op("scalar", I("activation", out=mag, in_=mag, func=AF.Exp), [wkb], [wkb])
    vop(I("tensor_tensor", out=th, in0=lim, in1=dt, op=ALU.mult), [spb, wkb], [wkb])
    for off, dst in ((0.0, sn), (0.5 * math.pi, cs)):
        vop(I("tensor_scalar", out=t1, in0=th, scalar1=off, scalar2=1.0 / TWO_PI, op0=ALU.add,
                                               op1=ALU.mult), [wkb], [wkb])
        vop(I("tensor_copy", out=wi, in_=t1), [wkb], [wib])
        vop(I("tensor_copy", out=kf, in_=wi), [wib], [wkb])
        vop(I("tensor_scalar", out=t1, in0=th, scalar1=off, scalar2=None, op0=ALU.add), [wkb], [wkb])
        vop(I("scalar_tensor_tensor", out=r, in0=kf, scalar=-TWO_PI, in1=t1, op0=ALU.mult, op1=ALU.add),
            [wkb], [wkb])
        P.op("scalar", I("activation", out=dst, in_=r, func=AF.Sin), [wkb], [wkb])
    vop(I("tensor_tensor", out=PR[:, 0, :], in0=mag, in1=cs, op=ALU.mult), [wkb], [prb])
    vop(I("tensor_tensor", out=PI[:, 0, :], in0=mag, in1=sn, op=ALU.mult), [wkb], [pib])
    vop(I("tensor_tensor", out=den, in0=lr, in1=lr, op=ALU.mult), [spb], [wkb])
    vop(I("tensor_tensor", out=tmpa, in0=lim, in1=lim, op=ALU.mult), [spb], [wkb])
    vop(I("tensor_tensor", out=den, in0=den, in1=tmpa, op=ALU.add), [wkb], [wkb])
    vop(I("reciprocal", out=den, in_=den), [wkb], [wkb])
    vop(I("tensor_scalar", out=nr, in0=PR[:, 0, :], scalar1=-1.0, scalar2=None, op0=ALU.add), [prb], [wkb])
    vop(I("tensor_tensor", out=tmpa, in0=nr, in1=lr, op=ALU.mult), [wkb, spb], [wkb])
    vop(I("tensor_tensor", out=tmpb, in0=PI[:, 0, :], in1=lim, op=ALU.mult), [pib, spb], [wkb])
    vop(I("tensor_tensor", out=tmpa, in0=tmpa, in1=tmpb, op=ALU.add), [wkb], [wkb])
    vop(I("tensor_tensor", out=zr, in0=tmpa, in1=den, op=ALU.mult), [wkb], [wkb])
    vop(I("tensor_tensor", out=tmpa, in0=PI[:, 0, :], in1=lr, op=ALU.mult), [pib, spb], [wkb])
    vop(I("tensor_tensor", out=tmpb, in0=nr, in1=lim, op=ALU.mult), [wkb, spb], [wkb])
    vop(I("tensor_tensor", out=tmpa, in0=tmpa, in1=tmpb, op=ALU.subtract), [wkb], [wkb])
    vop(I("tensor_tensor", out=zi, in0=tmpa, in1=den, op=ALU.mult), [wkb], [wkb])
    for k in range(10):
        vop(I("tensor_tensor", out=tmpa, in0=PR[:, k, :], in1=PR[:, k, :], op=ALU.mult), [prb], [wkb])
        vop(I("tensor_tensor", out=tmpb, in0=PI[:, k, :], in1=PI[:, k, :], op=ALU.mult), [pib], [wkb])
        vop(I("tensor_tensor", out=PR[:, k + 1, :], in0=tmpa, in1=tmpb, op=ALU.subtract), [wkb], [prb])
        vop(I("scalar_tensor_tensor", out=PI[:, k + 1, :], in0=PR[:, k, :], scalar=2.0, in1=PI[:, k, :],
                                                  op0=ALU.mult, op1=ALU.mult), [prb, pib], [pib])
    vop(I("tensor_scalar", out=NPI, in0=PI, scalar1=-1.0, scalar2=None, op0=ALU.mult), [pib], [npb])
    BBr, bbrb = A.alloc([128, 24, 16], F32)
    BBi, bbib = A.alloc([128, 24, 16], F32)
    tb3, tb3b = A.alloc([128, 24, 16], F32)
    bre = bb[:, 0, :].rearrange("p (a b) -> p a b", b=16)
    bim = bb[:, 1, :].rearrange("p (a b) -> p a b", b=16)
    cre = cc[:, 0, :].rearrange("p (a b) -> p a b", b=16)
    cim = cc[:, 1, :].rearrange("p (a b) -> p a b", b=16)
    zrb = zr.unsqueeze(2).to_broadcast([128, 24, 16])
    zib = zi.unsqueeze(2).to_broadcast([128, 24, 16])
    vop(I("tensor_tensor", out=BBr, in0=bre, in1=zrb, op=ALU.mult), [bbb, wkb], [bbrb])
    vop(I("tensor_tensor", out=tb3, in0=bim, in1=zib, op=ALU.mult), [bbb, wkb], [tb3b])
    vop(I("tensor_tensor", out=BBr, in0=BBr, in1=tb3, op=ALU.subtract), [bbrb, tb3b], [bbrb])
    vop(I("tensor_tensor", out=BBi, in0=bim, in1=zrb, op=ALU.mult), [bbb, wkb], [bbib])
    vop(I("tensor_tensor", out=tb3, in0=bre, in1=zib, op=ALU.mult), [bbb, wkb], [tb3b])
    vop(I("tensor_tensor", out=BBi, in0=BBi, in1=tb3, op=ALU.add), [bbib, tb3b], [bbib])
    vop(I("tensor_scalar", out=cc[:, 1, :], in0=cc[:, 1, :], scalar1=-1.0, scalar2=None, op0=ALU.mult),
        [ccb], [ccb])

    LBC = [A.alloc([128, 4, 128], F32) for _ in range(2)]
    Am, amb = A.alloc([128, 2, 128], F32, nbufs=2)
    us = [A.alloc([128, T], F32) for _ in range(2)]
    XR = [A.alloc([128, 3072], F32) for _ in range(2)]
    XI = [A.alloc([128, 3072], F32) for _ in range(2)]
    for (x, xb) in XR + XI:
        P.op("gpsimd", I("memset", x[:, 0:1024], 0.0), [], [xb])
    yv, yvb = A.alloc([128, T], F32)
    tt_, ttb = A.alloc([128, T], F32)
    stmp = [A.alloc([128, T], F32) for _ in range(2)]
    ygb = [A.alloc([128, T], BF16) for _ in range(6)]
    K.psn = 0
    psY = [K.ps[4 + i] for i in range(4)]

    def lo_psum():
        i = K.psn
        K.psn = (i + 1) % 4
        return K.ps[i]

    for ct in range(6):
        u, ub = us[ct % 2]
        P.dma("sync", u, K.usT.ap()[ct * 128:(ct + 1) * 128, :], reads=[dbuf(K, ("usT", ct))], writes=[ub])
        for jm in range(4):
            j = ct * 4 + jm
            L, Lb = LBC[j % 2]
            gm = K.gmask[:, jm * 8:(jm + 1) * 8].unsqueeze(2).to_broadcast([128, 8, 16])
            for k, (src, sb_) in enumerate(((BBr, bbrb), (BBi, bbib))):
                am = Am[:, k, :].rearrange("p (a b) -> p a b", b=16)
                vop(I("tensor_tensor",
                    out=am, in0=src[:, j, :].unsqueeze(1).to_broadcast([128, 8, 16]), in1=gm, op=ALU.mult),
                    [sb_, K.cstb], [amb[k]])
                pt, pb = lo_psum()
                P.op("tensor", I("transpose", out=pt[:, 0:128], in_=Am[:, k, :], identity=K.ident),
                     [amb[k], K.cstb], [pb])
                P.op("scalar", I("copy", out=L[:, k, :], in_=pt[:, 0:128]), [pb], [Lb])
            for k in range(2):
                lc = L[:, 2 + k, :].rearrange("p (a b) -> p a b", b=16)
                csrc = (cre if k == 0 else cim)
                vop(I("tensor_tensor",
                    out=lc, in0=csrc[:, j, :].unsqueeze(1).to_broadcast([128, 8, 16]), in1=gm, op=ALU.mult),
                    [ccb, K.cstb], [Lb])
            xr0, xr0b = XR[0]
            xi0, xi0b = XI[0]
            for tb in range(4):
                for k, (xd, xdb) in enumerate(((xr0, xr0b), (xi0, xi0b))):
                    ps, pb = lo_psum()
                    P.op("tensor", mm(ps, L[:, k, :], u[:, tb * 512:(tb + 1) * 512], True, True), [Lb, ub], [pb])
                    P.op("scalar", I("copy", out=xd[:, 1024 + tb * 512:1024 + (tb + 1) * 512],
                                                                         in_=ps), [pb], [xdb])
            for k in range(11):
                s_ = 1 << k
                a, b = k % 2, (k + 1) % 2
                sr, srb = XR[a]
                si, sib = XI[a]
                dr, drb = XR[b]
                di, dib = XI[b]
                pr = PR[:, k, j:j + 1]
                pi = PI[:, k, j:j + 1]
                npi = NPI[:, k, j:j + 1]
                lo, hi = 1024 - s_, 3072 - s_
                P.op("vector", I("scalar_tensor_tensor",
                    out=dr[:, 1024:3072], in0=sr[:, lo:hi], scalar=pr, in1=sr[:, 1024:3072], op0=ALU.mult, op1=ALU.add),
                    [srb, prb], [drb])
                P.op("vector", I("scalar_tensor_tensor",
                    out=dr[:, 1024:3072], in0=si[:, lo:hi], scalar=npi, in1=dr[:, 1024:3072], op0=ALU.mult, op1=ALU.add),
                    [sib, npb, drb], [drb])
                ta, tab = stmp[0]
                tbb_, tbbb = stmp[1]
                P.op("scalar", I("mul", out=ta, in_=si[:, lo:hi], mul=pr),
                     [sib, prb], [tab])
                P.op("scalar", I("mul", out=tbb_, in_=sr[:, lo:hi], mul=pi),
                     [srb, pib], [tbbb])
                P.op("gpsimd", I("tensor_tensor", out=di[:, 1024:3072], in0=ta,
                                                                              in1=si[:, 1024:3072], op=ALU.add),
                     [sib, tab], [dib])
                P.op("gpsimd", I("tensor_tensor", out=di[:, 1024:3072], in0=di[:, 1024:3072],
                                                                          in1=tbb_, op=ALU.add), [tbbb, dib], [dib])
            fr, frb = XR[1]
            fi, fib = XI[1]
            for tb in range(4):
                py, pyb = psY[tb]
                P.op("tensor", mm(py, L[:, 2, :], fr[:, 1024 + tb * 512:1024 + (tb + 1) * 512], jm == 0, False),
                     [Lb, frb], [pyb], inc=False)
                P.op("tensor", mm(py, L[:, 3, :], fi[:, 1024 + tb * 512:1024 + (tb + 1) * 512], False, jm == 3),
                     [Lb, fib], [pyb], inc=True)
        dsk = pvcol(K, li, PV_DSK + ct)
        for tb in range(4):
            py, pyb = psY[tb]
            sl = slice(tb * 512, (tb + 1) * 512)
            P.op("vector", I("scalar_tensor_tensor",
                out=yv[:, sl], in0=u[:, sl], scalar=dsk, in1=py, op0=ALU.mult, op1=ALU.add), [ub, pyb, K.pvb], [yvb])
        yg, ygbuf = ygb[ct]
        P.op("scalar", I("activation", out=tt_, in_=yv, func=AF.Square), [yvb], [ttb])
        P.op("vector", I("tensor_scalar", out=tt_, in0=tt_, scalar1=0.044715, scalar2=1.0, op0=ALU.mult,
                                                 op1=ALU.add), [ttb], [ttb])
        P.op("vector", I("tensor_tensor", out=tt_, in0=tt_, in1=yv, op=ALU.mult), [ttb, yvb], [ttb])
        P.op("scalar", I("activation", out=tt_, in_=tt_, func=AF.Sigmoid, scale=1.5957691216057308), [ttb], [ttb])
        P.op("vector", I("tensor_tensor", out=yg, in0=yv, in1=tt_, op=ALU.mult), [ttb, yvb], [ygbuf])
    K.psn = 0
    wg, wgb = A.alloc([128, 6, 768], BF16)
    P.dma("gpsimd", wg, K.w_glu.ap()[li].rearrange("(c p) n -> p c n", p=128), writes=[wgb])
    sgl = [A.alloc([128, 512], F32) for _ in range(2)]
    stg = [A.alloc([128, T], BF16) for _ in range(2)]
    n = 0
    for oc in range(6):
        sv, sb = stg[oc % 2]
        bcol = pvcol(K, li, PV_BGLU + oc)
        for tb in range(4):
            sl = slice(tb * 512, (tb + 1) * 512)
            ps, pb = psum(K)
            for ic in range(6):
                P.op("tensor", mm(ps, wg[:, ic, oc * 128:(oc + 1) * 128], ygb[ic][0][:, sl], ic == 0, ic == 5),
                     [wgb, ygb[ic][1]], [pb], inc=(ic == 5))
            sg, sgb = sgl[n % 2]
            n += 1
            P.op("scalar", I("activation", out=sg, in_=ps, func=AF.Sigmoid, bias=bcol),
                 [pb, K.pvb], [sgb])
            P.op("vector", I("tensor_tensor", out=sv[:, sl], in0=ygb[oc][0][:, sl],
                                                                               in1=sg, op=ALU.mult),
                 [sgb, ygb[oc][1]], [sb])
        P.dma("sync", K.yT.ap()[Y_S + oc * 128:Y_S + (oc + 1) * 128, :], sv, reads=[sb],
              writes=[dbuf(K, ("yT", (Y_S // 128) + oc))])
    A.release(m0)


def stage_mlstm(K, li):
    P, A = K.P, K.A
    m0 = A.mark()
    V = "vector"

    def vop(fn, reads, writes):
        P.op(V, fn, reads, writes)

    bgf, bgfb = A.alloc([128, 192], F32)
    decb, decbb = A.alloc([128, 64], F32)
    m1 = A.mark()
    pg, pgb = A.alloc([4, 4], F32)
    P.dma("sync", pg[:, 0:2], K.pgate.ap()[li], writes=[pgb])
    vop(I("tensor_scalar", out=pg[:, 2:3], in0=pg[:, 1:2], scalar1=-1.0, scalar2=None, op0=ALU.mult), [pgb], [pgb])
    ip, ipb = A.alloc([4, T], F32)
    fp, fpb = A.alloc([4, T], F32)
    bc, bcb = A.alloc([4, T], F32)
    gg, ggb = A.alloc([4, T], F32)
    sm, smb = A.alloc([4, 5, 16], F32)
    cmk, cmkb = A.alloc([4, T], F32)
    P.op("gpsimd", I("memset", cmk, 1.0), [], [cmkb])
    P.op("gpsimd", I("memset", cmk.rearrange("p (c t) -> p c t", t=128)[:, :, 0], 0.0), [cmkb], [cmkb])
    ifb = dbuf(K, ("ifT", 0))
    P.dma("sync", ip, K.ifT.ap()[0:4, :], reads=[ifb], writes=[ipb])
    P.dma("sync", fp, K.ifT.ap()[4:8, :], reads=[ifb], writes=[fpb])
    vop(I("tensor_scalar", out=ip, in0=ip, scalar1=pg[:, 0:1], scalar2=None, op0=ALU.add), [ipb, pgb], [ipb])
    P.op("scalar", I("activation", out=fp, in_=fp, func=AF.Exp, scale=-1.0, bias=pg[:, 2:3]), [fpb, pgb], [fpb])
    P.op("scalar", I("activation", out=fp, in_=fp, func=AF.Ln, bias=1.0), [fpb], [fpb])
    vop(I("tensor_scalar", out=fp, in0=fp, scalar1=-1.0, scalar2=None, op0=ALU.mult), [fpb], [fpb])
    vop(I("tensor_tensor_scan", out=bc, data0=cmk, data1=fp, initial=0.0, op0=ALU.mult, op1=ALU.add),
        [fpb, cmkb], [bcb])
    vop(I("tensor_tensor", out=ip, in0=ip, in1=bc, op=ALU.subtract), [ipb, bcb], [ipb])
    bc3 = bc.rearrange("p (c t) -> p c t", t=128)
    ip3 = ip.rearrange("p (c t) -> p c t", t=128)
    gg3 = gg.rearrange("p (c t) -> p c t", t=128)
    blast, gmax, Mc, Mprev, dec = [sm[:, i, :] for i in range(5)]
    vop(I("tensor_copy", out=blast, in_=bc3[:, :, 127]), [bcb], [smb])
    vop(I("tensor_tensor", out=gg3, in0=ip3, in1=blast.unsqueeze(2).to_broadcast([4, 16, 128]), op=ALU.add),
        [ipb, smb], [ggb])
    vop(I("tensor_reduce", out=gmax, in_=gg3, axis=AX.X, op=ALU.max), [ggb], [smb])
    vop(I("tensor_tensor_scan", out=Mc, data0=blast, data1=gmax, initial=0.0, op0=ALU.add, op1=ALU.max),
        [smb], [smb])
    vop(I("memset", Mprev[:, 0:1], 0.0), [], [smb])
    vop(I("tensor_copy", out=Mprev[:, 1:16], in_=Mc[:, 0:15]), [smb], [smb])
    mpb = Mprev.unsqueeze(2).to_broadcast([4, 16, 128])
    vop(I("tensor_tensor", out=ip3, in0=ip3, in1=mpb, op=ALU.subtract), [ipb, smb], [ipb])
    P.op("scalar", I("activation", out=ip, in_=ip, func=AF.Exp), [ipb], [ipb])
    vop(I("tensor_tensor", out=bc3, in0=bc3, in1=mpb, op=ALU.add), [bcb, smb], [bcb])
    P.op("scalar", I("activation", out=bc, in_=bc, func=AF.Exp, scale=-1.0), [bcb], [bcb])
    vop(I("tensor_tensor", out=gg3, in0=gg3, in1=Mc.unsqueeze(2).to_broadcast([4, 16, 128]), op=ALU.subtract),
        [ggb, smb], [ggb])
    P.op("scalar", I("activation", out=gg, in_=gg, func=AF.Exp), [ggb], [ggb])
    vop(I("tensor_tensor", out=dec, in0=blast, in1=Mprev, op=ALU.add), [smb], [smb])
    vop(I("tensor_tensor", out=dec, in0=dec, in1=Mc, op=ALU.subtract), [smb], [smb])
    P.op("scalar", I("activation", out=dec, in_=dec, func=AF.Exp), [smb], [smb])
    pT, pTb = psum(K)
    n = 0
    for q, (arr, ab) in enumerate(((ip, ipb), (gg, ggb), (bc, bcb))):
        for t in range(NT):
            n += 1
            P.op("tensor", I("transpose", out=pT[:, q * 64 + t * 4:q * 64 + t * 4 + 4],
                                                                    in_=arr[0:4, t * 128:(t + 1) * 128],
                                                                    identity=K.ident[0:4, 0:4]),
                 [ab, K.cstb], [pTb], inc=(n == 48))
    vop(I("tensor_copy", out=bgf, in_=pT[:, 0:192]), [pTb], [bgfb])
    dex, dexb = A.alloc([4, 4, 16], F32)
    vop(I("tensor_tensor", out=dex, in0=dec.unsqueeze(1).to_broadcast([4, 4, 16]),
                                  in1=K.ident[0:4, 0:4].unsqueeze(2).to_broadcast([4, 4, 16]), op=ALU.mult),
        [smb, K.cstb], [dexb])
    pD, pDb = psum(K)
    P.op("tensor", mm(pD[:, 0:64], K.ones[0:4, :], dex.rearrange("p a b -> p (a b)"), True, True), [dexb, K.cstb], [pDb])
    vop(I("tensor_copy", out=decb, in_=pD[:, 0:64]), [pDb], [decbb])
    A.release(m1)

    xm, xmb = A.alloc([128, T], F32)
    acc, accb = A.alloc([128, T], F32)
    xcb, xcbb = A.alloc([128, 2, T], BF16)
    skx, skxb = A.alloc([128, 2, T], BF16)
    sigo, sigob = A.alloc([128, 2, T], BF16)
    wq, wqb = A.alloc([128, 2, 256], BF16)
    wk, wkb_ = A.alloc([128, 2, 256], BF16)
    qT, qTb = A.alloc([128, 2, T], F32)
    kT, kTb = A.alloc([128, 2, T], F32)
    ktok, ktokb = A.alloc([128, NT, 256], F32)
    Va, Vab = A.alloc([128, NT, 257], F32)
    Cst, Cstb = A.alloc([128, 2, 257], F32)
    Wt = [A.alloc([128, 128], F32) for _ in range(2)]
    Vg = [A.alloc([128, 257], F32) for _ in range(2)]
    hs = [A.alloc([128, 256], F32) for _ in range(2)]
    junk, junkb = A.alloc([128, 256], F32)
    tmpv = [A.alloc([128, 128], F32) for _ in range(2)]
    smt, smtb = A.alloc([128, 2, 8], F32, nbufs=2)
    ymst = [A.alloc([128, 2, T], BF16) for _ in range(2)]
    P.op("gpsimd", I("memset", Va[:, :, 256:257], 1.0), [], [Vab])
    it = 0
    for h in range(4):
        ym, ymb = ymst[h % 2]
        for vt in range(2):
            ct = 2 * h + vt
            P.dma("sync", xm, K.xmT.ap()[ct * 128:(ct + 1) * 128, :], reads=[dbuf(K, ("xmT", ct))], writes=[xmb])
            P.dma("sync", sigo[:, vt, :], K.omT.ap()[ct * 128:(ct + 1) * 128, :], reads=[dbuf(K, ("omT", ct))],
                  writes=[sigob])
            w3 = pvcol(K, li, PV_CW + 3 * 8 + ct)
            cb_ = pvcol(K, li, PV_CB + ct)
            vop(I("tensor_scalar", out=acc, in0=xm, scalar1=w3, scalar2=cb_, op0=ALU.mult,
                                                         op1=ALU.add), [xmb, K.pvb], [accb])
            for jj in range(3):
                sh = 3 - jj
                wj = pvcol(K, li, PV_CW + jj * 8 + ct)
                vop(I("scalar_tensor_tensor", out=acc[:, sh:T], in0=xm[:, 0:T - sh], scalar=wj,
                                                                   in1=acc[:, sh:T], op0=ALU.mult, op1=ALU.add),
                    [xmb, accb, K.pvb], [accb])
            P.op("scalar", I("activation", out=xcb[:, vt, :], in_=acc, func=AF.Silu), [accb], [xcbb])
            skc = pvcol(K, li, PV_SK + ct)
            vop(I("tensor_scalar", out=skx[:, vt, :], in0=xcb[:, vt, :], scalar1=skc, scalar2=None,
                                                          op0=ALU.mult), [xcbb, K.pvb], [skxb])
        P.dma("gpsimd", wq, K.wq_m.ap()[li, h].rearrange("(c p) n -> p c n", p=128), writes=[wqb])
        P.dma("gpsimd", wk, K.wk_m.ap()[li, h].rearrange("(c p) n -> p c n", p=128), writes=[wkb_])
        P.dma("sync", Va[:, :, 0:256], K.vm.ap()[:, h * 256:(h + 1) * 256].rearrange("(t p) v -> p t v", p=128),
              reads=[dbuf(K, ("vm", tt, h // 2)) for tt in range(NT)], writes=[Vab])
        for et in range(2):
            for tb in range(4):
                sl = slice(tb * 512, (tb + 1) * 512)
                for (w_, wb_, dstT, dstb, sc) in ((wq, wqb, qT, qTb, 1.0), (wk, wkb_, kT, kTb, 0.0625)):
                    ps, pb = psum(K)
                    for dtt in range(2):
                        P.op("tensor", mm(ps, w_[:, dtt, et * 128:(et + 1) * 128], xcb[:, dtt, sl], dtt == 0, dtt == 1),
                             [wb_, xcbb], [pb], inc=(dtt == 1))
                    P.op("scalar", I("mul", out=dstT[:, et, sl], in_=ps,
                                                                                          mul=sc), [pb], [dstb])
        for tt in range(NT):
            ps, pb = psum(K)
            for dtt in range(2):
                P.op("tensor", mm(ps[:, 0:256], xcb[:, dtt, tt * 128:(tt + 1) * 128], wk[:, dtt, :], dtt == 0, dtt == 1),
                     [wkb_, xcbb], [pb], inc=(dtt == 1))
            P.op("scalar", I("mul", out=ktok[:, tt, :], in_=ps[:, 0:256], mul=0.0625), [pb], [ktokb])
        for c in range(NT):
            sl = slice(c * 128, (c + 1) * 128)
            W_, Wb = Wt[it % 2]
            Vg_, Vgb = Vg[it % 2]
            hs_, hsb = hs[it % 2]
            tv, tvb = tmpv[it % 2]
            s8 = smt[:, it % 2, :]
            s8b = smtb[it % 2]
            it += 1
            beta = bgf[:, 0 * 64 + c * 4 + h:0 * 64 + c * 4 + h + 1]
            gam = bgf[:, 1 * 64 + c * 4 + h:1 * 64 + c * 4 + h + 1]
            flo = bgf[:, 2 * 64 + c * 4 + h:2 * 64 + c * 4 + h + 1]
            psS, pbS = psum(K)
            for et in range(2):
                P.op("tensor", mm(psS[:, 0:128], kT[:, et, sl], qT[:, et, sl], et == 0, et == 1), [kTb, qTb], [pbS],
                     inc=(et == 1))
            vop(I("scalar_tensor_tensor", out=W_, in0=psS[:, 0:128], scalar=beta,
                                                                             in1=K.tri, op0=ALU.mult, op1=ALU.mult),
                [pbS, bgfb, K.cstb], [Wb])
            psN, pbN = psum(K)
            P.op("tensor", mm(psN[:, 0:257], W_, Va[:, c, :], True, c == 0), [Wb, Vab], [pbN], inc=(c == 0))
            if c > 0:
                for kt in range(2):
                    P.op("tensor", mm(psN[:, 0:257], qT[:, kt, sl], Cst[:, kt, :], False, kt == 1), [qTb, Cstb], [pbN],
                         inc=(kt == 1))
            a_, mx, r_, ssq, rr, rtot = [s8[:, i:i + 1] for i in range(6)]
            P.op("scalar", I("activation", out=a_, in_=psN[:, 256:257], func=AF.Abs), [pbN], [s8b])
            vop(I("tensor_tensor", out=mx, in0=a_, in1=flo, op=ALU.max), [s8b, bgfb], [s8b])
            vop(I("reciprocal", out=r_, in_=mx), [s8b], [s8b])
            P.op("gpsimd", I("memset", ssq, 0.0), [], [s8b])
            P.op("scalar", I("activation", out=junk, in_=psN[:, 0:256], func=AF.Square,
                                                                            scale=r_, accum_out=ssq),
                 [pbN, s8b], [junkb, s8b])
            P.op("scalar", I("activation", out=rr, in_=ssq, func=AF.Sqrt, scale=1.0 / 256, bias=EPS),
                 [s8b], [s8b])
            vop(I("reciprocal", out=rr, in_=rr), [s8b], [s8b])
            vop(I("tensor_tensor", out=rtot, in0=rr, in1=r_, op=ALU.mult), [s8b], [s8b])
            vop(I("tensor_scalar", out=hs_, in0=psN[:, 0:256], scalar1=rtot,
                                                                        scalar2=None, op0=ALU.mult), [pbN, s8b], [hsb])
            psT, pbT = psum(K)
            for vt in range(2):
                P.op("tensor", I("transpose", out=psT[:, vt * 128:(vt + 1) * 128],
                                                                               in_=hs_[:, vt * 128:(vt + 1) * 128],
                                                                               identity=K.ident),
                     [hsb, K.cstb], [pbT], inc=(vt == 1))
            for vt in range(2):
                ghc = pvcol(K, li, PV_GH + 2 * h + vt)
                vop(I("scalar_tensor_tensor",
                    out=tv, in0=psT[:, vt * 128:(vt + 1) * 128], scalar=ghc, in1=skx[:, vt, sl], op0=ALU.mult,
                    op1=ALU.add), [pbT, skxb, K.pvb], [tvb])
                vop(I("tensor_tensor", out=ym[:, vt, sl], in0=tv, in1=sigo[:, vt, sl],
                                                                          op=ALU.mult), [tvb, sigob], [ymb])
            if c < NT - 1:
                vop(I("tensor_scalar", out=Vg_, in0=Va[:, c, :], scalar1=gam, scalar2=None,
                                                                      op0=ALU.mult), [Vab, bgfb], [Vgb])
                for kt in range(2):
                    psU, pbU = psum(K)
                    P.op("tensor", mm(psU[:, 0:257], ktok[:, c, kt * 128:(kt + 1) * 128], Vg_, True, True),
                         [ktokb, Vgb], [pbU])
                    if c == 0:
                        P.op("scalar", I("copy", out=Cst[:, kt, :], in_=psU[:, 0:257]), [pbU], [Cstb])
                    else:
                        dcol = decb[:, h * 16 + c:h * 16 + c + 1]
                        vop(I("scalar_tensor_tensor",
                            out=Cst[:, kt, :], in0=Cst[:, kt, :], scalar=dcol, in1=psU[:, 0:257], op0=ALU.mult,
                            op1=ALU.add), [pbU, Cstb, decbb], [Cstb])
        for vt in range(2):
            ct = 2 * h + vt
            P.dma("sync", K.yT.ap()[Y_M + ct * 128:Y_M + (ct + 1) * 128, :], ym[:, vt, :], reads=[ymb],
                  writes=[dbuf(K, ("yT", (Y_M // 128) + ct))])
    A.release(m0)


def out_proj(K, li, aT, abufs, nch, wh, g, colw, W, src_t, dst_t, xst):
    P = K.P
    for cb in range(D // colw):
        wv, wb = W.load(wsrc(wh, li, 0, nch, cb * colw, colw), nch, colw)
        for tt in range(8):
            gt_ = g * 8 + tt
            xt, xb = xst.next()
            keys = xkeys(K, gt_, cb * colw, (cb + 1) * colw)
            P.dma("sync", xt[:, 0:colw], src_t.ap()[gt_ * 128:(gt_ + 1) * 128, cb * colw:(cb + 1) * colw], reads=keys,
                  writes=[xb])
            ps, pb = psum(K)
            for c in range(nch):
                P.op("tensor", mm(ps[:, 0:colw], aT[:, c, tt * 128:(tt + 1) * 128], wv[:, c, :], c == 0, c == nch - 1),
                     [wb] + abufs, [pb], inc=(c == nch - 1))
            P.op("vector", I("tensor_tensor", out=xt[:, 0:colw], in0=ps[:, 0:colw], in1=xt[:, 0:colw],
                                                                    op=ALU.add), [pb, xb], [xb])
            P.dma("sync", dst_t.ap()[gt_ * 128:(gt_ + 1) * 128, cb * colw:(cb + 1) * colw], xt[:, 0:colw], reads=[xb],
                  writes=keys)


def stage_merge(K, li):
    P, A = K.P, K.A
    src_t = K.x_in if li == 0 else K.xs
    for g in range(2):
        m0 = A.mark()
        mT, mTb = A.alloc([128, DC, 1024], BF16, nbufs=DC)
        W = WStream(K, 2, 22 * 512)
        m1 = A.mark()
        yTs, ytb = A.alloc([128, 22, 1024], BF16)
        P.dma("sync", yTs, K.yT.ap()[:, g * 1024:(g + 1) * 1024].rearrange("(c p) t -> p c t", p=128),
              reads=[dbuf(K, ("yT", c)) for c in range(22)], writes=[ytb])
        gts = [A.alloc([128, 3, 1024], BF16) for _ in range(2)]
        tmps = [[A.alloc([128, 512], F32) for _ in range(3)] for _ in range(2)]
        n = 0
        gview = K.gT.ap()[:, g * 1024:(g + 1) * 1024].rearrange("(b f p) t -> p b f t", b=3, p=128)
        for blk in range(4):
            v, b = W.slots[W.i]
            W.i = (W.i + 1) % len(W.slots)
            wv = v[:, 0:22 * 512].rearrange("p (a b) -> p a b", a=22)
            c0 = blk * 512
            P.dma("gpsimd", wv[:, 0:8, :], wsrc(K.w_br_a, li, 0, 8, c0, 512), writes=[b])
            P.dma("gpsimd", wv[:, 8:14, :], wsrc(K.w_br_s, li, 0, 6, c0, 512), writes=[b])
            P.dma("gpsimd", wv[:, 14:22, :], wsrc(K.w_br_m, li, 0, 8, c0, 512), writes=[b])
            for j in range(4):
                fb = blk * 4 + j
                gt, gtb = gts[fb % 2]
                P.dma("sync", gt, gview[:, :, fb, :], reads=[dbuf(K, ("gT", br * 16 + fb)) for br in range(3)],
                      writes=[gtb])
                for tb in range(2):
                    sl = slice(tb * 512, (tb + 1) * 512)
                    tm = tmps[n % 2]
                    n += 1
                    for br, (ca, cbb_) in enumerate(((0, 8), (8, 14), (14, 22))):
                        ps, pb = psum(K)
                        for c in range(ca, cbb_):
                            P.op("tensor", mm(ps, wv[:, c, j * 128:(j + 1) * 128], yTs[:, c, sl], c == ca, c == cbb_ - 1),
                                 [b, ytb], [pb], inc=(c == cbb_ - 1))
                        t_, t_b = tm[br]
                        P.op("vector", I("tensor_tensor",
                            out=t_, in0=ps, in1=gt[:, br, sl], op=ALU.mult), [pb, gtb], [t_b])
                    P.op("gpsimd", I("tensor_tensor", out=tm[0][0], in0=tm[0][0], in1=tm[1][0], op=ALU.add),
                         [tm[0][1], tm[1][1]], [tm[0][1]])
                    P.op("gpsimd", I("tensor_tensor", out=mT[:, fb, sl], in0=tm[0][0],
                                                                                  in1=tm[2][0], op=ALU.add),
                         [tm[0][1], tm[2][1]], [mTb[fb]])
        A.release(m1)
        xst = Stager(K, 4, [128, 512], F32)
        out_proj(K, li, mT, mTb, DC, K.w_out, g, 512, W, src_t, K.xs, xst)
        A.release(m0)


def stage_xattn(K, li):
    P, A = K.P, K.A
    mA = A.mark()
    kT, kTb = A.alloc([128, 16, 256], BF16)
    Vx, Vxb = A.alloc([128, 2, D], BF16)
    m0 = A.mark()
    W = WStream(K, 2, 16 * 512)
    sqs = [A.alloc([128, 4, 256], F32) for _ in range(1)]
    rt, rtb = A.alloc([128, 256], F32)
    memb = K.memTb
    for h in range(4):
        wv, wb = W.load(wsrc(K.w_xkv, li, 0, DC, h * 512, 512), DC, 512)
        sq, sqb = sqs[0]
        pss = []
        for et in range(4):
            ps, pb = psum(K)
            pss.append((ps, pb))
            for c in range(DC):
                P.op("tensor", mm(ps[:, 0:256], wv[:, c, et * 128:(et + 1) * 128], K.memT[:, c, :], c == 0, c == DC - 1),
                     [wb] + memb, [pb], inc=(c == DC - 1))
            P.op("scalar", I("activation", out=sq[:, et, :], in_=ps[:, 0:256], func=AF.Square),
                 [pb], [sqb])
        ps2, pb2 = psum(K)
        for et in range(4):
            P.op("tensor", mm(ps2[:, 0:256], K.ones, sq[:, et, :], et == 0, et == 3), [K.cstb, sqb], [pb2], inc=(et == 3))
        P.op("scalar", I("activation", out=rt, in_=ps2[:, 0:256], func=AF.Sqrt, scale=1.0 / 512, bias=EPS),
             [pb2], [rtb])
        P.op("vector", I("reciprocal", out=rt, in_=rt), [rtb], [rtb])
        for et in range(4):
            ps, pb = pss[et]
            g_ = K.pvd[:, li, 1 + et:2 + et]
            P.op("vector", I("scalar_tensor_tensor",
                out=kT[:, h * 4 + et, :], in0=ps[:, 0:256], scalar=g_, in1=rt, op0=ALU.mult, op1=ALU.mult),
                [pb, rtb, K.pvdb], [kTb])
    for cb in range(4):
        wv, wb = W.load(wsrc(K.w_xkv, li, 0, DC, D + cb * 512, 512), DC, 512)
        for mt in range(2):
            ps, pb = psum(K)
            for c in range(DC):
                P.op("tensor", mm(ps, K.memT[:, c, mt * 128:(mt + 1) * 128], wv[:, c, :], c == 0, c == DC - 1),
                     [wb] + memb, [pb], inc=(c == DC - 1))
            P.op("scalar", I("copy", out=Vx[:, mt, cb * 512:(cb + 1) * 512], in_=ps), [pb], [Vxb])
    A.release(m0)
    for g in range(2):
        m0 = A.mark()
        oT, oTb = A.alloc([128, DC, 1024], BF16, nbufs=DC)
        W = WStream(K, 2, 16 * 512)
        m1 = A.mark()
        hxT, hxb = A.alloc([128, DC, 1024], BF16, nbufs=8)

        def src_fn(tt, g=g):
            gt_ = g * 8 + tt
            return K.xs.ap()[gt_ * 128:(gt_ + 1) * 128, :], xkeys(K, gt_)

        norm_to_T(K, li, src_fn, 8, PV_NX, hxT, hxb)
        sq, sqb = A.alloc([128, 4, 512], F32)
        rt, rtb = A.alloc([128, 512], F32)
        qn, qnb = A.alloc([128, 4, 512], BF16)
        PT, PTb = A.alloc([128, 2, 512], BF16)
        rd, rdb = A.alloc([128, 512], F32)
        for h in range(4):
            wv, wb = W.load(wsrc(K.w_xq, li, 0, DC, h * 512, 512), DC, 512)
            for tb in range(2):
                sl = slice(tb * 512, (tb + 1) * 512)
                pss = []
                for et in range(4):
                    ps, pb = psum(K)
                    pss.append((ps, pb))
                    for c in range(DC):
                        P.op("tensor", mm(ps, wv[:, c, et * 128:(et + 1) * 128], hxT[:, c, sl], c == 0, c == DC - 1),
                             [wb] + hxb[tb * 4:tb * 4 + 4], [pb], inc=(c == DC - 1))
                    P.op("scalar", I("activation", out=sq[:, et, :], in_=ps, func=AF.Square),
                         [pb], [sqb])
                ps2, pb2 = psum(K)
                for et in range(4):
                    P.op("tensor", mm(ps2, K.ones, sq[:, et, :], et == 0, et == 3), [K.cstb, sqb], [pb2], inc=(et == 3))
                P.op("scalar", I("activation", out=rt, in_=ps2, func=AF.Sqrt, scale=1.0 / 512, bias=EPS),
                     [pb2], [rtb])
                P.op("vector", I("reciprocal", out=rt, in_=rt), [rtb], [rtb])
                for et in range(4):
                    ps, pb = pss[et]
                    g_ = pvcol(K, li, PV_GXQ + et)
                    P.op("vector", I("scalar_tensor_tensor",
                        out=qn[:, et, :], in0=ps, scalar=g_, in1=rt, op0=ALU.mult, op1=ALU.mult),
                        [pb, rtb, K.pvb], [qnb])
                for mt in range(2):
                    ps, pb = psum(K)
                    for et in range(4):
                        P.op("tensor", mm(ps, kT[:, h * 4 + et, mt * 128:(mt + 1) * 128], qn[:, et, :], et == 0, et == 3),
                             [kTb, qnb], [pb], inc=(et == 3))
                    P.op("scalar", I("activation", out=PT[:, mt, :], in_=ps, func=AF.Exp), [pb], [PTb])
                psD, pbD = psum(K)
                for mt in range(2):
                    P.op("tensor", mm(psD, K.onesb, PT[:, mt, :], mt == 0, mt == 1), [K.cbb, PTb], [pbD], inc=(mt == 1))
                P.op("vector", I("reciprocal", out=rd, in_=psD), [pbD], [rdb])
                for vt in range(4):
                    ps, pb = psum(K)
                    for mt in range(2):
                        P.op("tensor", mm(ps, Vx[:, mt, h * 512 + vt * 128:h * 512 + (vt + 1) * 128], PT[:, mt, :],
                                          mt == 0, mt == 1), [Vxb, PTb], [pb], inc=(mt == 1))
                    P.op("vector", I("tensor_tensor",
                        out=oT[:, h * 4 + vt, sl], in0=ps, in1=rd, op=ALU.mult), [pb, rdb], [oTb[h * 4 + vt]])
        A.release(m1)
        xst = Stager(K, 4, [128, 512], F32)
        out_proj(K, li, oT, oTb, DC, K.w_xo, g, 512, W, K.xs, K.xs, xst)
        A.release(m0)
    A.release(mA)


def stage_ffn(K, li):
    P, A = K.P, K.A
    dst_t = K.out if li == DEPTH - 1 else K.xs
    NF = FF // 128
    for g in range(2):
        m0 = A.mark()
        aT, aTb = A.alloc([128, NF, 1024], BF16, nbufs=NF)
        m1 = A.mark()
        hfT, hfb = A.alloc([128, DC, 1024], BF16, nbufs=8)

        def src_fn(tt, g=g):
            gt_ = g * 8 + tt
            return K.xs.ap()[gt_ * 128:(gt_ + 1) * 128, :], xkeys(K, gt_)

        norm_to_T(K, li, src_fn, 8, PV_NF, hfT, hfb)
        W = WStream(K, 4, 16 * 512)
        sgs = [A.alloc([128, 512], F32) for _ in range(2)]
        n = 0
        for blk in range(FF // 512):
            wg_, wgb_ = W.load(wsrc(K.w_gu, li, 0, DC, blk * 512, 512), DC, 512)
            wu_, wub_ = W.load(wsrc(K.w_gu, li, 0, DC, FF + blk * 512, 512), DC, 512)
            for j in range(4):
                fc = blk * 4 + j
                for tb in range(2):
                    sl = slice(tb * 512, (tb + 1) * 512)
                    psg, pbg = psum(K)
                    for c in range(DC):
                        P.op("tensor", mm(psg, wg_[:, c, j * 128:(j + 1) * 128], hfT[:, c, sl], c == 0, c == DC - 1),
                             [wgb_] + hfb[tb * 4:tb * 4 + 4], [pbg], inc=(c == DC - 1))
                    psu, pbu = psum(K)
                    for c in range(DC):
                        P.op("tensor", mm(psu, wu_[:, c, j * 128:(j + 1) * 128], hfT[:, c, sl], c == 0, c == DC - 1),
                             [wub_] + hfb[tb * 4:tb * 4 + 4], [pbu], inc=(c == DC - 1))
                    sg, sgb = sgs[n % 2]
                    n += 1
                    P.op("scalar", I("activation", out=sg, in_=psg, func=AF.Silu), [pbg], [sgb])
                    P.op("vector", I("tensor_tensor", out=aT[:, fc, sl], in0=psu, in1=sg,
                                                                                         op=ALU.mult),
                         [pbu, sgb], [aTb[fc]])
        A.release(m1)
        W2 = WStream(K, 2, NF * 256)
        xst = Stager(K, 4, [128, 256], F32)
        out_proj(K, li, aT, aTb, NF, K.w_down, g, 256, W2, K.xs, dst_t, xst)
        A.release(m0)


def host_layout(inputs):
    L = DEPTH
    f = lambda a: np.asarray(a, np.float32)
    pv = np.zeros((L, 128, NPV), np.float32)

    def colify(v):
        v = f(v)
        return v.reshape(-1, 128).T

    for li in range(L):
        pv[li, :, PV_NMIX:PV_NMIX + 16] = colify(inputs["norm_mix"][li])
        pv[li, :, PV_NX:PV_NX + 16] = colify(inputs["norm_x"][li])
        pv[li, :, PV_NF:PV_NF + 16] = colify(inputs["norm_ffn"][li])
        pv[li, :, PV_BG:PV_BG + 48] = colify(inputs["b_gate"][li])
        pv[li, :, PV_GQA:PV_GQA + 1] = colify(inputs["g_qa"][li])
        pv[li, :, PV_GKA:PV_GKA + 1] = colify(inputs["g_ka"][li])
        pv[li, :, PV_DSK:PV_DSK + 6] = colify(inputs["d_skip"][li])
        pv[li, :, PV_BGLU:PV_BGLU + 6] = colify(inputs["b_glu"][li])
        for j in range(4):
            pv[li, :, PV_CW + j * 8:PV_CW + j * 8 + 8] = colify(inputs["conv_w"][li][j])
        pv[li, :, PV_CB:PV_CB + 8] = colify(inputs["conv_b"][li])
        pv[li, :, PV_GH:PV_GH + 8] = colify(inputs["g_hm"][li])
        pv[li, :, PV_SK:PV_SK + 8] = colify(inputs["skip_m"][li])
        pv[li, :, PV_GXQ:PV_GXQ + 4] = colify(inputs["g_xq"][li])
        pv[li, :, PV_GXK:PV_GXK + 4] = colify(inputs["g_xk"][li])
        pv[li, :, PV_GMEM:PV_GMEM + 16] = colify(inputs["g_mem"])
    pg = np.stack([f(inputs["b_i"]), f(inputs["b_f"])], axis=-1)
    s5p = np.zeros((L, 128, 72), np.float32)
    s5b = np.zeros((L, 2, 128, 24 * 16), np.float32)
    s5c = np.zeros((L, 2, 128, 24 * 16), np.float32)
    for li in range(L):
        s5p[li, :, 0:24] = f(inputs["lam_re"][li]).reshape(24, 128).T
        s5p[li, :, 24:48] = f(inputs["lam_im"][li]).reshape(24, 128).T
        s5p[li, :, 48:72] = np.repeat(f(inputs["log_dt"][li]), 64).reshape(24, 128).T
        for k, nm in enumerate(("b_re", "b_im")):
            s5b[li, k] = f(inputs[nm][li]).reshape(24, 128, 16).transpose(1, 0, 2).reshape(128, 384)
        for k, nm in enumerate(("c_re", "c_im")):
            s5c[li, k] = f(inputs[nm][li]).reshape(24, 2, 16, 64).transpose(1, 3, 0, 2).reshape(128, 384)
    return dict(pvec=pv, pgate=np.ascontiguousarray(pg), s5p=s5p, s5b=s5b, s5c=s5c, consts=host_consts())


BIG = ("w_in", "rel_bias", "w_glu", "wq_m", "wk_m", "w_br_a", "w_br_s", "w_br_m", "w_out", "w_xq", "w_xkv", "w_xo",
       "w_gu", "w_down")

_NC_CACHE = {}


def make_in_maps(inputs, ncores=8):
    small = host_layout(inputs)
    shared = {k: np.ascontiguousarray(np.asarray(inputs[k], np.float32)) for k in BIG}
    shared.update(small)
    maps = []
    for c in range(ncores):
        b = c % 4
        m = dict(shared)
        m["x"] = np.ascontiguousarray(np.asarray(inputs["x"][b], np.float32))
        m["mem"] = np.ascontiguousarray(np.asarray(inputs["mem"][b], np.float32))
        maps.append(m)
    return maps


def kernel(**inputs):
    if "nc" not in _NC_CACHE:
        _NC_CACHE["nc"] = build_nc()
    nc = _NC_CACHE["nc"]
    maps = make_in_maps(inputs, 8)
    res = run_bass_kernel_spmd(nc, maps, core_ids=list(range(8)))
    out = np.stack([np.asarray(res.results[b]["out"], np.float32) for b in range(4)], axis=0)
    return out
```

```python
import contextlib
import math
import numpy as np
import concourse.bass as bass
import concourse.mybir as mybir
from concourse.bass_utils import run_bass_kernel_spmd

F32 = mybir.dt.float32
BF16 = mybir.dt.bfloat16
I32 = mybir.dt.int32
ALU = mybir.AluOpType
AF = mybir.ActivationFunctionType
AX = mybir.AxisListType

COMPUTE = ("tensor", "vector", "scalar", "gpsimd")
ALLENG = ("tensor", "vector", "scalar", "gpsimd", "sync")

T = 2048
D = 2048
NT = 16
DC = 16
INW = 13064
FF = 5632
EPS = 1e-6
DEPTH = 2
C_QA, C_KA, C_VA, C_US, C_XM, C_VM, C_OM, C_I, C_F, C_G = 0, 1024, 2048, 3072, 3840, 4864, 5888, 6912, 6916, 6920
Y_A, Y_S, Y_M = 0, 1024, 1792
PV_NMIX, PV_NX, PV_NF, PV_BG, PV_GQA, PV_GKA, PV_DSK, PV_BGLU, PV_CW, PV_CB, PV_GH, PV_SK, PV_GXQ, PV_GXK, PV_GMEM = (
    0, 16, 32, 48, 96, 97, 98, 104, 110, 142, 150, 158, 166, 170, 174)
NPV = 190


class Buf:
    __slots__ = ("name", "w", "re", "rd")

    def __init__(self, name=""):
        self.name = name
        self.w = None
        self.re = {}
        self.rd = []


class Op:
    __slots__ = ("eng", "fn", "waits", "inc", "cnt", "dma", "dsem", "dval")


class Prog:
    def __init__(self, nc, stack, n_dma_sems=(("sync", 24), ("gpsimd", 16)), same_eng_sync=True):
        self.nc = nc
        self.same_eng_sync = same_eng_sync
        self.ops = {e: [] for e in ALLENG}
        self.cnt = {e: 0 for e in COMPUTE}
        self.pending = {e: [] for e in COMPUTE}
        self.waited = {e: {} for e in ALLENG}
        self.esem = {e: stack.enter_context(nc.semaphore("es_" + e)) for e in COMPUTE}
        self.dsems = {}
        self.dnext = {}
        for e, n in n_dma_sems:
            self.dsems[e] = [[stack.enter_context(nc.semaphore("ds_%s%d" % (e, i))), 0, None] for i in range(n)]
            self.dnext[e] = 0
        self.nops = 0

    def _wait_for(self, op, d):
        if d is None or d is op:
            return
        if d.dma:
            key, val = d.dsem, d.dval
        else:
            if d.eng == op.eng and not op.dma:
                if d.eng == "tensor" or not self.same_eng_sync:
                    return
            if d.cnt is None:
                raise RuntimeError("dependency on op without milestone inc (%s)" % d.eng)
            key, val = self.esem[d.eng], d.cnt
        w = self.waited[op.eng]
        k = id(key)
        if w.get(k, 0) >= val:
            return
        w[k] = val
        for i, (s, v) in enumerate(op.waits):
            if s is key:
                op.waits[i] = (s, max(v, val))
                return
        op.waits.append((key, val))

    def op(self, eng, fn, reads=(), writes=(), inc=True, dma=False):
        o = Op()
        o.eng, o.fn, o.waits, o.inc, o.dma, o.cnt, o.dsem, o.dval = eng, fn, [], inc, dma, None, None, 0
        self.nops += 1
        for b in reads:
            self._wait_for(o, b.w)
        for b in writes:
            self._wait_for(o, b.w)
            for r in b.re.values():
                self._wait_for(o, r)
            for r in b.rd:
                self._wait_for(o, r)
        if dma:
            slots = self.dsems[eng]
            i = self.dnext[eng]
            self.dnext[eng] = (i + 1) % len(slots)
            slot = slots[i]
            if slot[2] is not None:
                self._wait_for(o, slot[2])
            slot[1] += 16
            slot[2] = o
            o.dsem, o.dval = slot[0], slot[1]
        elif inc:
            self.cnt[eng] += 1
            o.cnt = self.cnt[eng]
            for p in self.pending[eng]:
                p.cnt = o.cnt
            self.pending[eng] = []
        else:
            self.pending[eng].append(o)
        for b in reads:
            if dma:
                b.rd.append(o)
            else:
                b.re[eng] = o
        for b in writes:
            b.w = o
            b.re = {}
            b.rd = []
        self.ops[eng].append(o)
        return o

    def dma(self, eng, out, in_, reads=(), writes=(), **kw):
        return self.op(eng, I("dma_start", out=out, in_=in_, **kw), reads, writes, dma=True)

    def final_wait(self, eng="sync"):
        o = Op()
        o.eng, o.fn, o.waits, o.inc, o.dma, o.cnt, o.dsem, o.dval = eng, None, [], False, False, None, None, 0
        for e, slots in self.dsems.items():
            for s in slots:
                if s[2] is not None:
                    self._wait_for(o, s[2])
        for e in COMPUTE:
            for d in reversed(self.ops[e]):
                if not d.dma and d.cnt is not None:
                    self._wait_for(o, d)
                    break
        self.ops[eng].append(o)

    def emit(self):
        nc = self.nc
        with nc.Block() as block:
            for ename in ALLENG:
                ops = self.ops[ename]
                if not ops:
                    continue
                esem = self.esem.get(ename)

                def body(eng, ops=ops, esem=esem):
                    for o in ops:
                        for (s, v) in o.waits:
                            eng.wait_ge(s, v)
                        if o.fn is None:
                            continue
                        ins = o.fn(eng)
                        if o.dma:
                            ins.then_inc(o.dsem, 16)
                        elif o.inc:
                            ins.then_inc(esem, 1)

                getattr(block, ename)(body)


class Arena:
    def __init__(self, nc, stack, nbytes):
        self.cap = nbytes
        self.t = stack.enter_context(nc.sbuf_tensor("arena", [128, nbytes // 4], F32))
        self.top = 0
        self.regions = []

    def alloc(self, shape, dtype, nbufs=1):
        esz = 2 if dtype == BF16 else 4
        free = 1
        for s in shape[1:]:
            free *= s
        nb = (free * esz + 63) // 64 * 64
        start = self.top
        self.top += nb
        if self.top > self.cap:
            raise RuntimeError("arena overflow: %d > %d" % (self.top, self.cap))
        v = self.t[0:shape[0], start // 4:(start + nb) // 4]
        if dtype != F32:
            v = v.bitcast(dtype)
        v = v[:, 0:free]
        if len(shape) == 3:
            v = v.rearrange("p (a b) -> p a b", a=shape[1])
        elif len(shape) == 4:
            v = v.rearrange("p (a b c) -> p a b c", a=shape[1], b=shape[2])
        bufs = [Buf() for _ in range(nbufs)]
        keep = []
        for (s, e, ob) in self.regions:
            if s < start + nb and start < e:
                for o in ob:
                    for n in bufs:
                        if o.w is not None:
                            if o.w.dma:
                                n.rd.append(o.w)
                            else:
                                n.re[o.w.eng] = _later(n.re.get(o.w.eng), o.w)
                        for en, r in o.re.items():
                            n.re[en] = _later(n.re.get(en), r)
                        n.rd.extend(o.rd)
                if s >= start and e <= start + nb:
                    continue
            keep.append((s, e, ob))
        keep.append((start, start + nb, bufs))
        self.regions = keep
        return (v, bufs[0]) if nbufs == 1 else (v, bufs)

    def mark(self):
        return self.top

    def release(self, m):
        self.top = m


def _later(a, b):
    if a is None:
        return b
    if a.cnt is None:
        return a if b.cnt is not None else b
    if b.cnt is None:
        return b
    return a if a.cnt >= b.cnt else b


class Ctx:
    pass


def I(method, *a, **kw):
    return lambda e: getattr(e, method)(*a, **kw)


def mm(ps, lhsT, rhs, start, stop):
    return I("matmul", ps, lhsT=lhsT, rhs=rhs, start=start, stop=stop)


def build_nc(stages=("all",), debug=()):
    nc = bass.Bass("TRN2", target_bir_lowering=False)
    K = Ctx()
    K.nc = nc
    K.debug = debug

    def din(name, shape, dt=F32):
        return nc.dram_tensor(name, list(shape), dt, kind="ExternalInput")

    def dscr(name, shape, dt=F32):
        kind = "ExternalOutput" if name in debug else None
        if kind:
            return nc.dram_tensor(name, list(shape), dt, kind=kind)
        return nc.dram_tensor(name, list(shape), dt)

    K.x_in = din("x", [T, D])
    K.mem_in = din("mem", [256, D])
    K.w_in = din("w_in", [DEPTH, D, INW])
    K.rel_bias = din("rel_bias", [DEPTH, 8, 513])
    K.w_glu = din("w_glu", [DEPTH, 768, 768])
    K.wq_m = din("wq_m", [DEPTH, 4, 256, 256])
    K.wk_m = din("wk_m", [DEPTH, 4, 256, 256])
    K.w_br_a = din("w_br_a", [DEPTH, 1024, D])
    K.w_br_s = din("w_br_s", [DEPTH, 768, D])
    K.w_br_m = din("w_br_m", [DEPTH, 1024, D])
    K.w_out = din("w_out", [DEPTH, D, D])
    K.w_xq = din("w_xq", [DEPTH, D, D])
    K.w_xkv = din("w_xkv", [DEPTH, D, 2 * D])
    K.w_xo = din("w_xo", [DEPTH, D, D])
    K.w_gu = din("w_gu", [DEPTH, D, 2 * FF])
    K.w_down = din("w_down", [DEPTH, FF, D])
    K.pvec = din("pvec", [DEPTH, 128, NPV])
    K.pgate = din("pgate", [DEPTH, 4, 2])
    K.s5p = din("s5p", [DEPTH, 128, 24 * 3])
    K.s5b = din("s5b", [DEPTH, 2, 128, 24 * 16])
    K.s5c = din("s5c", [DEPTH, 2, 128, 24 * 16])
    K.consts = din("consts", [128, NCONST])
    K.out = nc.dram_tensor("out", [T, D], F32, kind="ExternalOutput")

    K.xs = dscr("xs", [T, D])
    K.qaT = dscr("qaT", [1024, T], BF16)
    K.kaT = dscr("kaT", [1024, T], BF16)
    K.va = dscr("va", [T, 1024], BF16)
    K.usT = dscr("usT", [768, T])
    K.xmT = dscr("xmT", [1024, T])
    K.vm = dscr("vm", [T, 1024])
    K.omT = dscr("omT", [1024, T], BF16)
    K.ifT = dscr("ifT", [8, T])
    K.gT = dscr("gT", [6144, T], BF16)
    K.yT = dscr("yT", [2816, T], BF16)
    K.ext = dscr("ext", [8, 1024])
    K.db = {}

    with contextlib.ExitStack() as st:
        P = Prog(nc, st)
        K.P = P
        K.A = Arena(nc, st, 206 * 1024)
        K.ps = []
        for i in range(8):
            t = st.enter_context(nc.psum_tensor("ps%d" % i, [128, 512], F32))
            K.ps.append((t[:, :], Buf("ps%d" % i)))
        K.psn = 0
        setup_consts(K)
        for li in range(DEPTH):
            if "all" in stages or ("s1", li) in stages:
                stage_inproj(K, li)
            if "all" in stages or ("att", li) in stages:
                stage_attn(K, li)
            if "all" in stages or ("s5", li) in stages:
                stage_s5(K, li)
            if "all" in stages or ("ml", li) in stages:
                stage_mlstm(K, li)
            if "all" in stages or ("mrg", li) in stages:
                stage_merge(K, li)
            if "all" in stages or ("xat", li) in stages:
                stage_xattn(K, li)
            if "all" in stages or ("ffn", li) in stages:
                stage_ffn(K, li)
        P.final_wait("sync")
        P.emit()
    return nc


def psum(K):
    i = K.psn
    K.psn = (i + 1) % 8
    return K.ps[i]


def dbuf(K, key):
    b = K.db.get(key)
    if b is None:
        b = Buf(str(key))
        K.db[key] = b
    return b


NCONST = 544


def host_consts():
    c = np.zeros((128, NCONST), np.float32)
    c[:, 0:128] = np.eye(128)
    c[:, 128:256] = np.eye(128)[::-1]
    c[:, 256:384] = 1.0
    c[:, 384:512] = np.triu(np.ones((128, 128)))
    for jm in range(4):
        for q in range(128):
            c[q, 512 + jm * 8 + 2 * jm + (1 if q >= 64 else 0)] = 1.0
    return c


def setup_consts(K):
    P, A = K.P, K.A
    K.cst, K.cstb = A.alloc([128, NCONST], F32)
    P.dma("sync", K.cst, K.consts.ap()[:, :], writes=[K.cstb])
    K.ident = K.cst[:, 0:128]
    K.antiI = K.cst[:, 128:256]
    K.ones = K.cst[:, 256:384]
    K.tri = K.cst[:, 384:512]
    K.gmask = K.cst[:, 512:544]
    K.identb, K.cbb = A.alloc([128, 256], BF16)
    P.op("vector", I("tensor_copy", out=K.identb[:, 0:128], in_=K.ident), [K.cstb], [K.cbb])
    P.op("vector", I("tensor_copy", out=K.identb[:, 128:256], in_=K.ones), [K.cstb], [K.cbb])
    K.onesb = K.identb[:, 128:256]
    K.identbb = K.identb[:, 0:128]
    K.pv, K.pvb = A.alloc([128, DEPTH, NPV], F32)
    for li in range(DEPTH):
        P.dma("sync", K.pv[:, li, :], K.pvec.ap()[li], writes=[K.pvb])
    K.pvd, K.pvdb = A.alloc([128, DEPTH, 8], F32)
    for li in range(DEPTH):
        P.op("vector", I("tensor_scalar_mul", out=K.pvd[:, li, 0:1], in0=K.pv[:, li, PV_GQA:PV_GQA + 1],
                                                              scalar1=128 ** -0.5), [K.pvb], [K.pvdb])
        P.op("vector", I("tensor_scalar_mul", out=K.pvd[:, li, 1:5], in0=K.pv[:, li, PV_GXK:PV_GXK + 4],
                                                              scalar1=512 ** -0.5), [K.pvb], [K.pvdb])
    K.memT, mtb = A.alloc([128, DC, 256], BF16, nbufs=2)
    K.memTb = mtb

    def msrc(tt):
        return K.mem_in.ap()[tt * 128:(tt + 1) * 128, :], []

    norm_to_T(K, 0, msrc, 2, PV_GMEM, K.memT, mtb)
    K.base_mark = A.mark()


def xkeys(K, tt, c0=0, c1=D):
    return [dbuf(K, ("xs", tt, q)) for q in range(c0 // 256, (c1 + 255) // 256)]


def pvcol(K, li, col, n=1):
    return K.pv[:, li, col:col + n]


def norm_to_T(K, li, src_fn, ntiles, gcol, hT, hbufs, pvl=None):
    P, A = K.P, K.A
    m = A.mark()
    NB_ = 3
    xts = [A.alloc([128, D], F32) for _ in range(NB_)]
    hbs = [A.alloc([128, D], BF16) for _ in range(NB_)]
    junk, jb = A.alloc([128, D], BF16)
    st, stb = A.alloc([128, 2 * NB_], F32, nbufs=NB_)
    pl = li if pvl is None else pvl
    for tt in range(ntiles):
        src, sbuf = src_fn(tt)
        xt, xb = xts[tt % NB_]
        hb, hbb = hbs[tt % NB_]
        ss = st[:, (tt % NB_) * 2:(tt % NB_) * 2 + 1]
        rs = st[:, (tt % NB_) * 2 + 1:(tt % NB_) * 2 + 2]
        sb = stb[tt % NB_]
        P.dma("sync", xt, src, reads=sbuf, writes=[xb])
        P.op("gpsimd", I("memset", ss, 0.0), [], [sb])
        P.op("scalar", I("activation", out=junk, in_=xt, func=AF.Square, accum_out=ss),
             [xb, sb], [jb, sb])
        P.op("scalar", I("activation", out=rs, in_=ss, func=AF.Sqrt, scale=1.0 / D, bias=EPS),
             [sb], [sb])
        P.op("vector", I("reciprocal", out=rs, in_=rs), [sb], [sb])
        P.op("scalar", I("mul", out=hb, in_=xt, mul=rs), [xb, sb], [hbb])
        for half in range(2):
            pt, pb = psum(K)
            ptb = pt[:, :].bitcast(BF16).rearrange("p (a b) -> p a b", a=8)
            for cc in range(8):
                c = half * 8 + cc
                P.op("tensor", I("transpose", out=ptb[:, cc, :],
                                                                                 in_=hb[:, c * 128:(c + 1) * 128],
                                                                                 identity=K.identbb),
                     [hbb, K.cbb], [pb], inc=(cc == 7))
            g = K.pv[:, pl, gcol + half * 8:gcol + half * 8 + 8].unsqueeze(2).to_broadcast([128, 8, 128])
            P.op("vector", I("tensor_tensor",
                out=hT[:, half * 8:half * 8 + 8, tt * 128:(tt + 1) * 128], in0=ptb, in1=g, op=ALU.mult),
                [pb, K.pvb], [hbufs[tt]])
    A.release(m)


class WStream:
    def __init__(self, K, nslots, nelem):
        self.K = K
        self.slots = [K.A.alloc([128, nelem], BF16) for _ in range(nslots)]
        self.i = 0
        self.nelem = nelem

    def load(self, src3d, nch, ncols):
        v, b = self.slots[self.i]
        self.i = (self.i + 1) % len(self.slots)
        assert nch * ncols <= self.nelem
        dst = v[:, 0:nch * ncols].rearrange("p (a b) -> p a b", a=nch)
        self.K.P.dma("gpsimd", dst, src3d, writes=[b])
        return dst, b


def wsrc(h, li, r0, nch, c0, ncols):
    return h.ap()[li, r0:r0 + nch * 128, c0:c0 + ncols].rearrange("(c p) n -> p c n", p=128)


class Stager:
    def __init__(self, K, n, shape, dtype):
        self.t = [K.A.alloc(shape, dtype) for _ in range(n)]
        self.i = 0

    def next(self):
        r = self.t[self.i]
        self.i = (self.i + 1) % len(self.t)
        return r


def stage_inproj(K, li):
    P, A, nc = K.P, K.A, K.nc
    m0 = A.mark()
    hT, hbufs = A.alloc([128, DC, T], BF16, nbufs=NT)
    xsrc = K.x_in if li == 0 else K.xs

    def src_fn(tt):
        return xsrc.ap()[tt * 128:(tt + 1) * 128, :], xkeys(K, tt)

    norm_to_T(K, li, src_fn, NT, PV_NMIX, hT, hbufs)
    if "dbg_hT" in K.debug:
        dh = K.nc.dram_tensor("dbg_hT", [128, DC * T], BF16, kind="ExternalOutput")
        P.dma("sync", dh.ap()[:, :], hT.rearrange("p a b -> p (a b)"), reads=hbufs)
        dp = K.nc.dram_tensor("dbg_pv", [128, DEPTH * NPV], F32, kind="ExternalOutput")
        P.dma("sync", dp.ap()[:, :], K.pv.rearrange("p a b -> p (a b)"), reads=[K.pvb])
        return
    W = WStream(K, 3, 16 * 512)
    stg32 = Stager(K, 2, [128, T], F32)
    stg16 = Stager(K, 2, [128, T], BF16)
    sq_t = [A.alloc([128, 512], F32) for _ in range(2)]
    rt_t = [A.alloc([128, 512], F32) for _ in range(2)]
    cnt = [0]

    def fm_block(c0, ncols, evac, post):
        wv, wb = W.load(wsrc(K.w_in, li, 0, DC, c0, ncols), DC, ncols)
        nj = (ncols + 127) // 128
        for j in range(nj):
            mcols = min(128, ncols - j * 128)
            for tb in range(4):
                ps, pb = psum(K)
                for c in range(DC):
                    P.op("tensor", mm(ps[0:mcols, :], wv[:, c, j * 128:j * 128 + mcols], hT[:, c, tb * 512:(tb + 1) * 512],
                                      c == 0, c == DC - 1), [wb] + hbufs[tb * 4:tb * 4 + 4], [pb], inc=(c == DC - 1))
                evac(j, tb, ps, pb, mcols)
            post(j)

    for which, c_base, dst, gsrc in (("q", C_QA, K.qaT, lambda: K.pvd[:, li, 0:1]),
                                     ("k", C_KA, K.kaT, lambda: pvcol(K, li, PV_GKA))):
        for blk in range(2):
            cur = {}

            def evac(j, tb, ps, pb, mcols, cur=cur, gsrc=gsrc):
                if tb == 0:
                    cur["s"] = stg16.next()
                sv, sb = cur["s"]
                k = cnt[0]
                cnt[0] += 1
                sq, sqb = sq_t[k % 2]
                rt, rtb = rt_t[k % 2]
                P.op("scalar", I("activation", out=sq, in_=ps, func=AF.Square), [pb], [sqb])
                ps2, pb2 = psum(K)
                P.op("tensor", mm(ps2, K.ones, sq, True, True), [K.cstb, sqb], [pb2])
                P.op("scalar", I("activation", out=rt, in_=ps2, func=AF.Sqrt, scale=1.0 / 128, bias=EPS),
                     [pb2], [rtb])
                P.op("vector", I("reciprocal", out=rt, in_=rt), [rtb], [rtb])
                g = gsrc()
                P.op("vector", I("scalar_tensor_tensor", out=sv[:, tb * 512:(tb + 1) * 512], in0=ps, scalar=g,
                                                                in1=rt, op0=ALU.mult, op1=ALU.mult),
                     [pb, rtb, K.pvb, K.pvdb], [sb])

            def post(j, cur=cur, dst=dst, blk=blk):
                sv, sb = cur["s"]
                hrow = (blk * 4 + j) * 128
                P.dma("sync", dst.ap()[hrow:hrow + 128, :], sv, reads=[sb], writes=[dbuf(K, (dst.name, blk * 4 + j))])

            fm_block(c_base + blk * 512, 512, evac, post)

    def fm_plain(c_base, ncols_total, dst, dt, kind, key):
        nblk = (ncols_total + 511) // 512
        for blk in range(nblk):
            nc_ = min(512, ncols_total - blk * 512)
            cur = {}

            def evac(j, tb, ps, pb, mcols, cur=cur, blk=blk):
                if tb == 0:
                    cur["s"] = (stg32 if dt == F32 else stg16).next()
                sv, sb = cur["s"]
                o = sv[0:mcols, tb * 512:(tb + 1) * 512]
                if kind == "copy":
                    if (tb % 2) == 0:
                        P.op("scalar", I("copy", out=o, in_=ps[0:mcols, :]), [pb], [sb])
                    else:
                        P.op("vector", I("tensor_copy", out=o, in_=ps[0:mcols, :]), [pb], [sb])
                elif kind == "sig":
                    P.op("scalar", I("activation", out=o, in_=ps[0:mcols, :], func=AF.Sigmoid), [pb], [sb])
                elif kind == "sigb":
                    bcol = pvcol(K, li, PV_BG + blk * 4 + j)
                    P.op("scalar", I("activation", out=o, in_=ps[0:mcols, :], func=AF.Sigmoid, bias=bcol),
                         [pb, K.pvb], [sb])

            def post(j, cur=cur, blk=blk, nc_=nc_):
                sv, sb = cur["s"]
                r0 = blk * 512 + j * 128
                mcols = min(128, nc_ - j * 128)
                P.dma("sync", dst.ap()[r0:r0 + mcols, :], sv[0:mcols, :], reads=[sb],
                      writes=[dbuf(K, (key, r0 // 128))])

            fm_block(c_base + blk * 512, nc_, evac, post)

    fm_plain(C_US, 768, K.usT, F32, "copy", "usT")
    fm_plain(C_XM, 1024, K.xmT, F32, "copy", "xmT")
    fm_plain(C_OM, 1024, K.omT, BF16, "sig", "omT")
    fm_plain(C_I, 8, K.ifT, F32, "copy", "ifT")
    fm_plain(C_G, 6144, K.gT, BF16, "sigb", "gT")

    stt32 = Stager(K, 3, [128, 512], F32)
    stt16 = Stager(K, 3, [128, 512], BF16)
    for c_base, dst, dt, key in ((C_VA, K.va, BF16, "va"), (C_VM, K.vm, F32, "vm")):
        for blk in range(2):
            wv, wb = W.load(wsrc(K.w_in, li, 0, DC, c_base + blk * 512, 512), DC, 512)
            for tt in range(NT):
                ps, pb = psum(K)
                for c in range(DC):
                    P.op("tensor", mm(ps, hT[:, c, tt * 128:(tt + 1) * 128], wv[:, c, :], c == 0, c == DC - 1),
                         [wb, hbufs[tt]], [pb], inc=(c == DC - 1))
                sv, sb = (stt32 if dt == F32 else stt16).next()
                if tt % 2 == 0:
                    P.op("scalar", I("copy", out=sv, in_=ps), [pb], [sb])
                else:
                    P.op("vector", I("tensor_copy", out=sv, in_=ps), [pb], [sb])
                P.dma("sync", dst.ap()[tt * 128:(tt + 1) * 128, blk * 512:(blk + 1) * 512], sv, reads=[sb],
                      writes=[dbuf(K, (key, tt, blk))])
    A.release(m0)


def stage_attn(K, li):
    P, A = K.P, K.A
    m0 = A.mark()
    E, Eb = A.alloc([8, 1024], F32)
    P.op("gpsimd", I("memset", E, 0.0), [], [Eb])
    P.dma("sync", E[:, 384:897], K.rel_bias.ap()[li], writes=[Eb])
    P.op("vector", I("tensor_copy", out=E[:, 0:384], in_=E[:, 384:385].to_broadcast([8, 384])), [Eb], [Eb])
    extb = dbuf(K, "ext")
    P.dma("sync", K.ext.ap()[:, :], E, reads=[Eb], writes=[extb])
    qs = [A.alloc([128, T], BF16) for _ in range(2)]
    ks = [A.alloc([128, T], BF16) for _ in range(2)]
    vs = [A.alloc([128, NT, 128], BF16) for _ in range(2)]
    Ls = [A.alloc([128, 5, 128], F32) for _ in range(2)]
    PTs = [A.alloc([128, 5, 128], BF16) for _ in range(2)]
    sts = [A.alloc([128, T], BF16) for _ in range(2)]
    rds = [A.alloc([128, 128], F32) for _ in range(2)]
    for (pt, ptb) in PTs:
        P.op("gpsimd", I("memset", pt, 0.0), [], [ptb])
    it = 0
    for h in range(8):
        qT, qb = qs[h % 2]
        kT, kb = ks[h % 2]
        V, vb = vs[h % 2]
        Lt, lb = Ls[h % 2]
        sv, sb = sts[h % 2]
        P.dma("sync", qT, K.qaT.ap()[h * 128:(h + 1) * 128, :], reads=[dbuf(K, ("qaT", h))], writes=[qb])
        P.dma("sync", kT, K.kaT.ap()[h * 128:(h + 1) * 128, :], reads=[dbuf(K, ("kaT", h))], writes=[kb])
        P.dma("sync", V, K.va.ap()[:, h * 128:(h + 1) * 128].rearrange("(t p) d -> p t d", p=128),
              reads=[dbuf(K, ("va", tt, h // 4)) for tt in range(NT)], writes=[vb])
        for m in range(5):
            P.dma("sync", Lt[:, m, :], bass.AP(K.ext, h * 1024 + 513 - 128 * m, [[1, 128], [1, 128]]),
                  reads=[extb], writes=[lb])
        for j in range(NT):
            PT, ptb = PTs[it % 2]
            rd, rdb = rds[it % 2]
            it += 1
            psA, pbA = psum(K)
            mmax = min(4, j)
            qsl = qT[:, j * 128:(j + 1) * 128]
            for m in range(min(3, mmax) + 1):
                kt = j - m
                tgt = psA[:, m * 128:(m + 1) * 128]
                P.op("tensor", mm(tgt, kT[:, kt * 128:(kt + 1) * 128], qsl, True, False), [kb, qb], [pbA], inc=False)
                P.op("tensor", mm(tgt, Lt[:, m, :], K.antiI, False, True), [lb, K.cstb], [pbA],
                     inc=(m == min(3, mmax)))
            if mmax == 4:
                psB, pbB = psum(K)
                kt = j - 4
                tgt = psB[:, 0:128]
                P.op("tensor", mm(tgt, kT[:, kt * 128:(kt + 1) * 128], qsl, True, False), [kb, qb], [pbB], inc=False)
                P.op("tensor", mm(tgt, Lt[:, 4, :], K.antiI, False, True), [lb, K.cstb], [pbB])
            P.op("scalar", I("activation", out=PT[0:64, 0, :], in_=psA[0:64, 0:128], func=AF.Exp),
                 [pbA], [ptb])
            P.op("scalar", I("activation", out=PT[64:128, 0, 64:128], in_=psA[64:128, 64:128],
                                                                   func=AF.Exp), [pbA], [ptb])
            n13 = min(3, mmax)
            if n13 >= 1:
                P.op("scalar", I("activation",
                    out=PT[:, 1:n13 + 1, :], in_=psA[:, 128:128 * (n13 + 1)].rearrange("p (a b) -> p a b", b=128),
                    func=AF.Exp), [pbA], [ptb])
            if mmax == 4:
                P.op("scalar", I("activation", out=PT[64:128, 4, :], in_=psB[64:128, 0:128],
                                                                       func=AF.Exp), [pbB], [ptb])
                P.op("scalar", I("activation", out=PT[0:64, 4, 0:64], in_=psB[0:64, 0:64],
                                                                       func=AF.Exp), [pbB], [ptb])
            psO, pbO = psum(K)
            for m in range(mmax + 1):
                kt = j - m
                P.op("tensor", mm(psO[:, 0:128], V[:, kt, :], PT[:, m, :], m == 0, m == mmax), [vb, ptb], [pbO], inc=False)
            for m in range(mmax + 1):
                P.op("tensor", mm(psO[:, 128:256], K.onesb, PT[:, m, :], m == 0, m == mmax), [K.cbb, ptb], [pbO],
                     inc=(m == mmax))
            P.op("vector", I("reciprocal", out=rd, in_=psO[:, 128:256]), [pbO], [rdb])
            P.op("vector", I("tensor_tensor",
                out=sv[:, j * 128:(j + 1) * 128], in0=psO[:, 0:128], in1=rd, op=ALU.mult), [pbO, rdb], [sb])
        P.dma("sync", K.yT.ap()[Y_A + h * 128:Y_A + (h + 1) * 128, :], sv, reads=[sb],
              writes=[dbuf(K, ("yT", (Y_A // 128) + h))])
    A.release(m0)


def stage_s5(K, li):
    P, A = K.P, K.A
    m0 = A.mark()
    V = "vector"
    sp, spb = A.alloc([128, 72], F32)
    P.dma("sync", sp, K.s5p.ap()[li], writes=[spb])
    bb, bbb = A.alloc([128, 2, 384], F32)
    cc, ccb = A.alloc([128, 2, 384], F32)
    for k in range(2):
        P.dma("sync", bb[:, k, :], K.s5b.ap()[li, k], writes=[bbb])
        P.dma("sync", cc[:, k, :], K.s5c.ap()[li, k], writes=[ccb])
    lr, lim, ldt = sp[:, 0:24], sp[:, 24:48], sp[:, 48:72]
    wk, wkb = A.alloc([128, 16, 24], F32)
    wi, wib = A.alloc([128, 24], I32)
    PR, prb = A.alloc([128, 11, 24], F32)
    PI, pib = A.alloc([128, 11, 24], F32)
    NPI, npb = A.alloc([128, 11, 24], F32)
    dt, mag, th, t1, kf, r, sn, cs, den, nr, tmpa, tmpb, zr, zi = [wk[:, i, :] for i in range(14)]
    TWO_PI = 2 * math.pi

    def vop(fn, reads, writes):
        P.op(V, fn, reads, writes)

    P.op("scalar", I("activation", out=dt, in_=ldt, func=AF.Exp), [spb], [wkb])
    vop(I("tensor_tensor", out=mag, in0=lr, in1=dt, op=ALU.mult), [spb, wkb], [wkb])
    P.op("scalar", I("activation", out=mag, in_=mag, func=AF.Exp), [wkb], [wkb])
    vop(I("tensor_tensor", out=th, in0=lim, in1=dt, op=ALU.mult), [spb, wkb], [wkb])
    for off, dst in ((0.0, sn), (0.5 * math.pi, cs)):
        vop(I("tensor_scalar", out=t1, in0=th, scalar1=off, scalar2=1.0 / TWO_PI, op0=ALU.add,
                                               op1=ALU.mult), [wkb], [wkb])
        vop(I("tensor_copy", out=wi, in_=t1), [wkb], [wib])
        vop(I("tensor_copy", out=kf, in_=wi), [wib], [wkb])
        vop(I("tensor_scalar", out=t1, in0=th, scalar1=off, scalar2=None, op0=ALU.add), [wkb], [wkb])
        vop(I("scalar_tensor_tensor", out=r, in0=kf, scalar=-TWO_PI, in1=t1, op0=ALU.mult, op1=ALU.add),
            [wkb], [wkb])
        P.op("scalar", I("activation", out=dst, in_=r, func=AF.Sin), [wkb], [wkb])
    vop(I("tensor_tensor", out=PR[:, 0, :], in0=mag, in1=cs, op=ALU.mult), [wkb], [prb])
    vop(I("tensor_tensor", out=PI[:, 0, :], in0=mag, in1=sn, op=ALU.mult), [wkb], [pib])
    vop(I("tensor_tensor", out=den, in0=lr, in1=lr, op=ALU.mult), [spb], [wkb])
    vop(I("tensor_tensor", out=tmpa, in0=lim, in1=lim, op=ALU.mult), [spb], [wkb])
    vop(I("tensor_tensor", out=den, in0=den, in1=tmpa, op=ALU.add), [wkb], [wkb])
    vop(I("reciprocal", out=den, in_=den), [wkb], [wkb])
    vop(I("tensor_scalar", out=nr, in0=PR[:, 0, :], scalar1=-1.0, scalar2=None, op0=ALU.add), [prb], [wkb])
    vop(I("tensor_tensor", out=tmpa, in0=nr, in1=lr, op=ALU.mult), [wkb, spb], [wkb])
    vop(I("tensor_tensor", out=tmpb, in0=PI[:, 0, :], in1=lim, op=ALU.mult), [pib, spb], [wkb])
    vop(I("tensor_tensor", out=tmpa, in0=tmpa, in1=tmpb, op=ALU.add), [wkb], [wkb])
    vop(I("tensor_tensor", out=zr, in0=tmpa, in1=den, op=ALU.mult), [wkb], [wkb])
    vop(I("tensor_tensor", out=tmpa, in0=PI[:, 0, :], in1=lr, op=ALU.mult), [pib, spb], [wkb])
    vop(I("tensor_tensor", out=tmpb, in0=nr, in1=lim, op=ALU.mult), [wkb, spb], [wkb])
    vop(I("tensor_tensor", out=tmpa, in0=tmpa, in1=tmpb, op=ALU.subtract), [wkb], [wkb])
    vop(I("tensor_tensor", out=zi, in0=tmpa, in1=den, op=ALU.mult), [wkb], [wkb])
    UC, ucb = A.alloc([128, 11, 24], F32)
    US, usb_ = A.alloc([128, 11, 24], F32)
    NUS, nusb = A.alloc([128, 11, 24], F32)
    vop(I("tensor_copy", out=UC[:, 0, :], in_=cs), [wkb], [ucb])
    vop(I("tensor_copy", out=US[:, 0, :], in_=sn), [wkb], [usb_])
    for k in range(10):
        vop(I("tensor_tensor", out=tmpa, in0=UC[:, k, :], in1=UC[:, k, :], op=ALU.mult), [ucb], [wkb])
        vop(I("tensor_tensor", out=tmpb, in0=US[:, k, :], in1=US[:, k, :], op=ALU.mult), [usb_], [wkb])
        vop(I("tensor_tensor", out=UC[:, k + 1, :], in0=tmpa, in1=tmpb, op=ALU.subtract), [wkb], [ucb])
        vop(I("scalar_tensor_tensor", out=US[:, k + 1, :], in0=UC[:, k, :], scalar=2.0, in1=US[:, k, :],
              op0=ALU.mult, op1=ALU.mult), [ucb, usb_], [usb_])
    vop(I("tensor_scalar", out=NUS, in0=US, scalar1=-1.0, scalar2=None, op0=ALU.mult), [usb_], [nusb])
    BBr, bbrb = A.alloc([128, 24, 16], F32)
    BBi, bbib = A.alloc([128, 24, 16], F32)
    tb3, tb3b = A.alloc([128, 24, 16], F32)
    bre = bb[:, 0, :].rearrange("p (a b) -> p a b", b=16)
    bim = bb[:, 1, :].rearrange("p (a b) -> p a b", b=16)
    cre = cc[:, 0, :].rearrange("p (a b) -> p a b", b=16)
    cim = cc[:, 1, :].rearrange("p (a b) -> p a b", b=16)
    zrb = zr.unsqueeze(2).to_broadcast([128, 24, 16])
    zib = zi.unsqueeze(2).to_broadcast([128, 24, 16])
    vop(I("tensor_tensor", out=BBr, in0=bre, in1=zrb, op=ALU.mult), [bbb, wkb], [bbrb])
    vop(I("tensor_tensor", out=tb3, in0=bim, in1=zib, op=ALU.mult), [bbb, wkb], [tb3b])
    vop(I("tensor_tensor", out=BBr, in0=BBr, in1=tb3, op=ALU.subtract), [bbrb, tb3b], [bbrb])
    vop(I("tensor_tensor", out=BBi, in0=bim, in1=zrb, op=ALU.mult), [bbb, wkb], [bbib])
    vop(I("tensor_tensor", out=tb3, in0=bre, in1=zib, op=ALU.mult), [bbb, wkb], [tb3b])
    vop(I("tensor_tensor", out=BBi, in0=BBi, in1=tb3, op=ALU.add), [bbib, tb3b], [bbib])
    vop(I("tensor_scalar", out=cc[:, 1, :], in0=cc[:, 1, :], scalar1=-1.0, scalar2=None, op0=ALU.mult),
        [ccb], [ccb])

    LBC = [A.alloc([128, 4, 128], F32) for _ in range(2)]
    Am, amb = A.alloc([128, 2, 128], F32, nbufs=2)
    us = [A.alloc([128, T], F32) for _ in range(2)]
    B0 = [A.alloc([128, T], F32) for _ in range(2)]
    B1 = [A.alloc([128, T], F32) for _ in range(2)]
    ECs = [A.alloc([128, T], F32) for _ in range(2)]
    ESs = [A.alloc([128, T], F32) for _ in range(2)]
    VR, vrb = A.alloc([128, T], F32)
    VI, vib = A.alloc([128, T], F32)
    WR, wrb = A.alloc([128, T], F32)
    WI, wib_ = A.alloc([128, T], F32)
    TT, ttb2 = A.alloc([128, T], F32)
    yv, yvb = VR, vrb
    tt_, ttb = VI, vib
    ygb = [A.alloc([128, T], BF16) for _ in range(6)]

    def table_steps(j):
        EC, ecb = ECs[j % 2]
        ES, esb = ESs[j % 2]
        steps = []

        def init():
            P.op("gpsimd", I("memset", EC[:, 0:1], 1.0), [], [ecb])
            P.op("gpsimd", I("memset", ES[:, 0:1], 0.0), [], [esb])
        steps.append(init)
        for k in range(11):
            def step(k=k):
                n_ = 1 << k
                uc = UC[:, k, j:j + 1]
                us_ = US[:, k, j:j + 1]
                nus = NUS[:, k, j:j + 1]
                P.op("scalar", I("mul", out=EC[:, n_:2 * n_], in_=EC[:, 0:n_], mul=uc), [ecb, ucb], [ecb])
                P.op("scalar", I("mul", out=ES[:, n_:2 * n_], in_=ES[:, 0:n_], mul=uc), [esb, ucb], [esb])
                P.op("vector", I("scalar_tensor_tensor", out=EC[:, n_:2 * n_], in0=ES[:, 0:n_], scalar=nus,
                                 in1=EC[:, n_:2 * n_], op0=ALU.mult, op1=ALU.add), [esb, nusb, ecb], [ecb])
                P.op("vector", I("scalar_tensor_tensor", out=ES[:, n_:2 * n_], in0=EC[:, 0:n_], scalar=us_,
                                 in1=ES[:, n_:2 * n_], op0=ALU.mult, op1=ALU.add), [ecb, usb_, esb], [esb])
            steps.append(step)
        return steps

    pend = []
    for f_ in table_steps(0):
        f_()

    def bigop(fn, reads, writes):
        vop(fn, reads, writes)
        if pend:
            pend.pop(0)()
    K.psn = 0
    psY = [K.ps[4 + i] for i in range(4)]

    def lo_psum():
        i = K.psn
        K.psn = (i + 1) % 4
        return K.ps[i]

    for ct in range(6):
        u, ub = us[ct % 2]
        P.dma("sync", u, K.usT.ap()[ct * 128:(ct + 1) * 128, :], reads=[dbuf(K, ("usT", ct))], writes=[ub])
        for jm in range(4):
            j = ct * 4 + jm
            L, Lb = LBC[j % 2]
            gm = K.gmask[:, jm * 8:(jm + 1) * 8].unsqueeze(2).to_broadcast([128, 8, 16])
            for k, (src, sb_) in enumerate(((BBr, bbrb), (BBi, bbib))):
                am = Am[:, k, :].rearrange("p (a b) -> p a b", b=16)
                vop(I("tensor_tensor",
                    out=am, in0=src[:, j, :].unsqueeze(1).to_broadcast([128, 8, 16]), in1=gm, op=ALU.mult),
                    [sb_, K.cstb], [amb[k]])
                pt, pb = lo_psum()
                P.op("tensor", I("transpose", out=pt[:, 0:128], in_=Am[:, k, :], identity=K.ident),
                     [amb[k], K.cstb], [pb])
                P.op("scalar", I("copy", out=L[:, k, :], in_=pt[:, 0:128]), [pb], [Lb])
            for k in range(2):
                lc = L[:, 2 + k, :].rearrange("p (a b) -> p a b", b=16)
                csrc = (cre if k == 0 else cim)
                vop(I("tensor_tensor",
                    out=lc, in0=csrc[:, j, :].unsqueeze(1).to_broadcast([128, 8, 16]), in1=gm, op=ALU.mult),
                    [ccb, K.cstb], [Lb])
            b0, b0b = B0[j % 2]
            b1, b1b = B1[j % 2]
            for tb in range(4):
                for k, (xd, xdb) in enumerate(((b0, b0b), (b1, b1b))):
                    ps, pb = lo_psum()
                    P.op("tensor", mm(ps, L[:, k, :], u[:, tb * 512:(tb + 1) * 512], True, True), [Lb, ub], [pb])
                    P.op("scalar", I("copy", out=xd[:, tb * 512:(tb + 1) * 512], in_=ps), [pb], [xdb])
            EC, ecb = ECs[j % 2]
            ES, esb = ESs[j % 2]
            while pend:
                pend.pop(0)()
            if j + 1 < 24:
                pend.extend(table_steps(j + 1))
            bigop(I("tensor_tensor", out=VR, in0=b0, in1=EC, op=ALU.mult), [b0b, ecb], [vrb])
            bigop(I("tensor_tensor", out=TT, in0=b1, in1=ES, op=ALU.mult), [b1b, esb], [ttb2])
            bigop(I("tensor_tensor", out=VR, in0=VR, in1=TT, op=ALU.add), [vrb, ttb2], [vrb])
            bigop(I("tensor_tensor", out=VI, in0=b1, in1=EC, op=ALU.mult), [b1b, ecb], [vib])
            bigop(I("tensor_tensor", out=TT, in0=b0, in1=ES, op=ALU.mult), [b0b, esb], [ttb2])
            bigop(I("tensor_tensor", out=VI, in0=VI, in1=TT, op=ALU.subtract), [vib, ttb2], [vib])
            magb = mag[:, j:j + 1].to_broadcast([128, T])
            bigop(I("tensor_tensor_scan", out=WR, data0=magb, data1=VR, initial=0.0, op0=ALU.mult, op1=ALU.add),
                [vrb, wkb], [wrb])
            bigop(I("tensor_tensor_scan", out=WI, data0=magb, data1=VI, initial=0.0, op0=ALU.mult, op1=ALU.add),
                [vib, wkb], [wib_])
            bigop(I("tensor_tensor", out=b0, in0=WR, in1=EC, op=ALU.mult), [wrb, ecb], [b0b])
            bigop(I("tensor_tensor", out=TT, in0=WI, in1=ES, op=ALU.mult), [wib_, esb], [ttb2])
            bigop(I("tensor_tensor", out=b0, in0=b0, in1=TT, op=ALU.subtract), [b0b, ttb2], [b0b])
            bigop(I("tensor_tensor", out=b1, in0=WI, in1=EC, op=ALU.mult), [wib_, ecb], [b1b])
            bigop(I("tensor_tensor", out=TT, in0=WR, in1=ES, op=ALU.mult), [wrb, esb], [ttb2])
            bigop(I("tensor_tensor", out=b1, in0=b1, in1=TT, op=ALU.add), [b1b, ttb2], [b1b])
            for tb in range(4):
                py, pyb = psY[tb]
                P.op("tensor", mm(py, L[:, 2, :], b0[:, tb * 512:(tb + 1) * 512], jm == 0, False),
                     [Lb, b0b], [pyb], inc=False)
                P.op("tensor", mm(py, L[:, 3, :], b1[:, tb * 512:(tb + 1) * 512], False, jm == 3),
                     [Lb, b1b], [pyb], inc=True)
        dsk = pvcol(K, li, PV_DSK + ct)
        for tb in range(4):
            py, pyb = psY[tb]
            sl = slice(tb * 512, (tb + 1) * 512)
            P.op("vector", I("scalar_tensor_tensor",
                out=yv[:, sl], in0=u[:, sl], scalar=dsk, in1=py, op0=ALU.mult, op1=ALU.add), [ub, pyb, K.pvb], [yvb])
        yg, ygbuf = ygb[ct]
        P.op("scalar", I("activation", out=tt_, in_=yv, func=AF.Square), [yvb], [ttb])
        P.op("vector", I("tensor_scalar", out=tt_, in0=tt_, scalar1=0.044715, scalar2=1.0, op0=ALU.mult,
                                                 op1=ALU.add), [ttb], [ttb])
        P.op("vector", I("tensor_tensor", out=tt_, in0=tt_, in1=yv, op=ALU.mult), [ttb, yvb], [ttb])
        P.op("scalar", I("activation", out=tt_, in_=tt_, func=AF.Sigmoid, scale=1.5957691216057308), [ttb], [ttb])
        P.op("vector", I("tensor_tensor", out=yg, in0=yv, in1=tt_, op=ALU.mult), [ttb, yvb], [ygbuf])
    K.psn = 0
    wg, wgb = A.alloc([128, 6, 768], BF16)
    P.dma("gpsimd", wg, K.w_glu.ap()[li].rearrange("(c p) n -> p c n", p=128), writes=[wgb])
    sgl = [A.alloc([128, 512], F32) for _ in range(2)]
    stg = [A.alloc([128, T], BF16) for _ in range(2)]
    n = 0
    for oc in range(6):
        sv, sb = stg[oc % 2]
        bcol = pvcol(K, li, PV_BGLU + oc)
        for tb in range(4):
            sl = slice(tb * 512, (tb + 1) * 512)
            ps, pb = psum(K)
            for ic in range(6):
                P.op("tensor", mm(ps, wg[:, ic, oc * 128:(oc + 1) * 128], ygb[ic][0][:, sl], ic == 0, ic == 5),
                     [wgb, ygb[ic][1]], [pb], inc=(ic == 5))
            sg, sgb = sgl[n % 2]
            n += 1
            P.op("scalar", I("activation", out=sg, in_=ps, func=AF.Sigmoid, bias=bcol),
                 [pb, K.pvb], [sgb])
            P.op("vector", I("tensor_tensor", out=sv[:, sl], in0=ygb[oc][0][:, sl],
                                                                               in1=sg, op=ALU.mult),
                 [sgb, ygb[oc][1]], [sb])
        P.dma("sync", K.yT.ap()[Y_S + oc * 128:Y_S + (oc + 1) * 128, :], sv, reads=[sb],
              writes=[dbuf(K, ("yT", (Y_S // 128) + oc))])
    A.release(m0)


def stage_mlstm(K, li):
    P, A = K.P, K.A
    m0 = A.mark()
    V = "vector"

    def vop(fn, reads, writes):
        P.op(V, fn, reads, writes)

    bgf, bgfb = A.alloc([128, 192], F32)
    decb, decbb = A.alloc([128, 64], F32)
    m1 = A.mark()
    pg, pgb = A.alloc([4, 4], F32)
    P.dma("sync", pg[:, 0:2], K.pgate.ap()[li], writes=[pgb])
    vop(I("tensor_scalar", out=pg[:, 2:3], in0=pg[:, 1:2], scalar1=-1.0, scalar2=None, op0=ALU.mult), [pgb], [pgb])
    ip, ipb = A.alloc([4, T], F32)
    fp, fpb = A.alloc([4, T], F32)
    bc, bcb = A.alloc([4, T], F32)
    gg, ggb = A.alloc([4, T], F32)
    sm, smb = A.alloc([4, 5, 16], F32)
    cmk, cmkb = A.alloc([4, T], F32)
    P.op("gpsimd", I("memset", cmk, 1.0), [], [cmkb])
    P.op("gpsimd", I("memset", cmk.rearrange("p (c t) -> p c t", t=128)[:, :, 0], 0.0), [cmkb], [cmkb])
    ifb = dbuf(K, ("ifT", 0))
    P.dma("sync", ip, K.ifT.ap()[0:4, :], reads=[ifb], writes=[ipb])
    P.dma("sync", fp, K.ifT.ap()[4:8, :], reads=[ifb], writes=[fpb])
    vop(I("tensor_scalar", out=ip, in0=ip, scalar1=pg[:, 0:1], scalar2=None, op0=ALU.add), [ipb, pgb], [ipb])
    P.op("scalar", I("activation", out=fp, in_=fp, func=AF.Exp, scale=-1.0, bias=pg[:, 2:3]), [fpb, pgb], [fpb])
    P.op("scalar", I("activation", out=fp, in_=fp, func=AF.Ln, bias=1.0), [fpb], [fpb])
    vop(I("tensor_scalar", out=fp, in0=fp, scalar1=-1.0, scalar2=None, op0=ALU.mult), [fpb], [fpb])
    vop(I("tensor_tensor_scan", out=bc, data0=cmk, data1=fp, initial=0.0, op0=ALU.mult, op1=ALU.add),
        [fpb, cmkb], [bcb])
    vop(I("tensor_tensor", out=ip, in0=ip, in1=bc, op=ALU.subtract), [ipb, bcb], [ipb])
    bc3 = bc.rearrange("p (c t) -> p c t", t=128)
    ip3 = ip.rearrange("p (c t) -> p c t", t=128)
    gg3 = gg.rearrange("p (c t) -> p c t", t=128)
    blast, gmax, Mc, Mprev, dec = [sm[:, i, :] for i in range(5)]
    vop(I("tensor_copy", out=blast, in_=bc3[:, :, 127]), [bcb], [smb])
    vop(I("tensor_tensor", out=gg3, in0=ip3, in1=blast.unsqueeze(2).to_broadcast([4, 16, 128]), op=ALU.add),
        [ipb, smb], [ggb])
    vop(I("tensor_reduce", out=gmax, in_=gg3, axis=AX.X, op=ALU.max), [ggb], [smb])
    vop(I("tensor_tensor_scan", out=Mc, data0=blast, data1=gmax, initial=0.0, op0=ALU.add, op1=ALU.max),
        [smb], [smb])
    vop(I("memset", Mprev[:, 0:1], 0.0), [], [smb])
    vop(I("tensor_copy", out=Mprev[:, 1:16], in_=Mc[:, 0:15]), [smb], [smb])
    mpb = Mprev.unsqueeze(2).to_broadcast([4, 16, 128])
    vop(I("tensor_tensor", out=ip3, in0=ip3, in1=mpb, op=ALU.subtract), [ipb, smb], [ipb])
    P.op("scalar", I("activation", out=ip, in_=ip, func=AF.Exp), [ipb], [ipb])
    vop(I("tensor_tensor", out=bc3, in0=bc3, in1=mpb, op=ALU.add), [bcb, smb], [bcb])
    P.op("scalar", I("activation", out=bc, in_=bc, func=AF.Exp, scale=-1.0), [bcb], [bcb])
    vop(I("tensor_tensor", out=gg3, in0=gg3, in1=Mc.unsqueeze(2).to_broadcast([4, 16, 128]), op=ALU.subtract),
        [ggb, smb], [ggb])
    P.op("scalar", I("activation", out=gg, in_=gg, func=AF.Exp), [ggb], [ggb])
    vop(I("tensor_tensor", out=dec, in0=blast, in1=Mprev, op=ALU.add), [smb], [smb])
    vop(I("tensor_tensor", out=dec, in0=dec, in1=Mc, op=ALU.subtract), [smb], [smb])
    P.op("scalar", I("activation", out=dec, in_=dec, func=AF.Exp), [smb], [smb])
    pT, pTb = psum(K)
    n = 0
    for q, (arr, ab) in enumerate(((ip, ipb), (gg, ggb), (bc, bcb))):
        for t in range(NT):
            n += 1
            P.op("tensor", I("transpose", out=pT[:, q * 64 + t * 4:q * 64 + t * 4 + 4],
                                                                    in_=arr[0:4, t * 128:(t + 1) * 128],
                                                                    identity=K.ident[0:4, 0:4]),
                 [ab, K.cstb], [pTb], inc=(n == 48))
    vop(I("tensor_copy", out=bgf, in_=pT[:, 0:192]), [pTb], [bgfb])
    dex, dexb = A.alloc([4, 4, 16], F32)
    vop(I("tensor_tensor", out=dex, in0=dec.unsqueeze(1).to_broadcast([4, 4, 16]),
                                  in1=K.ident[0:4, 0:4].unsqueeze(2).to_broadcast([4, 4, 16]), op=ALU.mult),
        [smb, K.cstb], [dexb])
    pD, pDb = psum(K)
    P.op("tensor", mm(pD[:, 0:64], K.ones[0:4, :], dex.rearrange("p a b -> p (a b)"), True, True), [dexb, K.cstb], [pDb])
    vop(I("tensor_copy", out=decb, in_=pD[:, 0:64]), [pDb], [decbb])
    A.release(m1)

    xm, xmb = A.alloc([128, T], F32)
    acc, accb = A.alloc([128, T], F32)
    xcb, xcbb = A.alloc([128, 2, T], BF16)
    skx, skxb = A.alloc([128, 2, T], BF16)
    sigo, sigob = A.alloc([128, 2, T], BF16)
    wq, wqb = A.alloc([128, 2, 256], BF16)
    wk, wkb_ = A.alloc([128, 2, 256], BF16)
    qT, qTb = A.alloc([128, 2, T], F32)
    kT, kTb = A.alloc([128, 2, T], F32)
    ktok, ktokb = A.alloc([128, NT, 256], F32)
    Va, Vab = A.alloc([128, NT, 257], F32)
    Cst, Cstb = A.alloc([128, 2, 257], F32)
    Wt = [A.alloc([128, 128], F32) for _ in range(2)]
    Vg = [A.alloc([128, 257], F32) for _ in range(2)]
    hs = [A.alloc([128, 256], F32) for _ in range(2)]
    junk, junkb = A.alloc([128, 256], F32)
    tmpv = [A.alloc([128, 128], F32) for _ in range(2)]
    smt, smtb = A.alloc([128, 2, 8], F32, nbufs=2)
    ymst = [A.alloc([128, 2, T], BF16) for _ in range(2)]
    P.op("gpsimd", I("memset", Va[:, :, 256:257], 1.0), [], [Vab])
    it = 0
    for h in range(4):
        ym, ymb = ymst[h % 2]
        for vt in range(2):
            ct = 2 * h + vt
            P.dma("sync", xm, K.xmT.ap()[ct * 128:(ct + 1) * 128, :], reads=[dbuf(K, ("xmT", ct))], writes=[xmb])
            P.dma("sync", sigo[:, vt, :], K.omT.ap()[ct * 128:(ct + 1) * 128, :], reads=[dbuf(K, ("omT", ct))],
                  writes=[sigob])
            w3 = pvcol(K, li, PV_CW + 3 * 8 + ct)
            cb_ = pvcol(K, li, PV_CB + ct)
            vop(I("tensor_scalar", out=acc, in0=xm, scalar1=w3, scalar2=cb_, op0=ALU.mult,
                                                         op1=ALU.add), [xmb, K.pvb], [accb])
            for jj in range(3):
                sh = 3 - jj
                wj = pvcol(K, li, PV_CW + jj * 8 + ct)
                vop(I("scalar_tensor_tensor", out=acc[:, sh:T], in0=xm[:, 0:T - sh], scalar=wj,
                                                                   in1=acc[:, sh:T], op0=ALU.mult, op1=ALU.add),
                    [xmb, accb, K.pvb], [accb])
            P.op("scalar", I("activation", out=xcb[:, vt, :], in_=acc, func=AF.Silu), [accb], [xcbb])
            skc = pvcol(K, li, PV_SK + ct)
            vop(I("tensor_scalar", out=skx[:, vt, :], in0=xcb[:, vt, :], scalar1=skc, scalar2=None,
                                                          op0=ALU.mult), [xcbb, K.pvb], [skxb])
        P.dma("gpsimd", wq, K.wq_m.ap()[li, h].rearrange("(c p) n -> p c n", p=128), writes=[wqb])
        P.dma("gpsimd", wk, K.wk_m.ap()[li, h].rearrange("(c p) n -> p c n", p=128), writes=[wkb_])
        P.dma("sync", Va[:, :, 0:256], K.vm.ap()[:, h * 256:(h + 1) * 256].rearrange("(t p) v -> p t v", p=128),
              reads=[dbuf(K, ("vm", tt, h // 2)) for tt in range(NT)], writes=[Vab])
        for et in range(2):
            for tb in range(4):
                sl = slice(tb * 512, (tb + 1) * 512)
                for (w_, wb_, dstT, dstb, sc) in ((wq, wqb, qT, qTb, 1.0), (wk, wkb_, kT, kTb, 0.0625)):
                    ps, pb = psum(K)
                    for dtt in range(2):
                        P.op("tensor", mm(ps, w_[:, dtt, et * 128:(et + 1) * 128], xcb[:, dtt, sl], dtt == 0, dtt == 1),
                             [wb_, xcbb], [pb], inc=(dtt == 1))
                    P.op("scalar", I("mul", out=dstT[:, et, sl], in_=ps,
                                                                                          mul=sc), [pb], [dstb])
        for tt in range(NT):
            ps, pb = psum(K)
            for dtt in range(2):
                P.op("tensor", mm(ps[:, 0:256], xcb[:, dtt, tt * 128:(tt + 1) * 128], wk[:, dtt, :], dtt == 0, dtt == 1),
                     [wkb_, xcbb], [pb], inc=(dtt == 1))
            P.op("scalar", I("mul", out=ktok[:, tt, :], in_=ps[:, 0:256], mul=0.0625), [pb], [ktokb])
        for c in range(NT):
            sl = slice(c * 128, (c + 1) * 128)
            W_, Wb = Wt[it % 2]
            Vg_, Vgb = Vg[it % 2]
            hs_, hsb = hs[it % 2]
            tv, tvb = tmpv[it % 2]
            s8 = smt[:, it % 2, :]
            s8b = smtb[it % 2]
            it += 1
            beta = bgf[:, 0 * 64 + c * 4 + h:0 * 64 + c * 4 + h + 1]
            gam = bgf[:, 1 * 64 + c * 4 + h:1 * 64 + c * 4 + h + 1]
            flo = bgf[:, 2 * 64 + c * 4 + h:2 * 64 + c * 4 + h + 1]
            psS, pbS = psum(K)
            for et in range(2):
                P.op("tensor", mm(psS[:, 0:128], kT[:, et, sl], qT[:, et, sl], et == 0, et == 1), [kTb, qTb], [pbS],
                     inc=(et == 1))
            vop(I("scalar_tensor_tensor", out=W_, in0=psS[:, 0:128], scalar=beta,
                                                                             in1=K.tri, op0=ALU.mult, op1=ALU.mult),
                [pbS, bgfb, K.cstb], [Wb])
            psN, pbN = psum(K)
            P.op("tensor", mm(psN[:, 0:257], W_, Va[:, c, :], True, c == 0), [Wb, Vab], [pbN], inc=(c == 0))
            if c > 0:
                for kt in range(2):
                    P.op("tensor", mm(psN[:, 0:257], qT[:, kt, sl], Cst[:, kt, :], False, kt == 1), [qTb, Cstb], [pbN],
                         inc=(kt == 1))
            a_, mx, r_, ssq, rr, rtot = [s8[:, i:i + 1] for i in range(6)]
            P.op("scalar", I("activation", out=a_, in_=psN[:, 256:257], func=AF.Abs), [pbN], [s8b])
            vop(I("tensor_tensor", out=mx, in0=a_, in1=flo, op=ALU.max), [s8b, bgfb], [s8b])
            vop(I("reciprocal", out=r_, in_=mx), [s8b], [s8b])
            P.op("gpsimd", I("memset", ssq, 0.0), [], [s8b])
            P.op("scalar", I("activation", out=junk, in_=psN[:, 0:256], func=AF.Square,
                                                                            scale=r_, accum_out=ssq),
                 [pbN, s8b], [junkb, s8b])
            P.op("scalar", I("activation", out=rr, in_=ssq, func=AF.Sqrt, scale=1.0 / 256, bias=EPS),
                 [s8b], [s8b])
            vop(I("reciprocal", out=rr, in_=rr), [s8b], [s8b])
            vop(I("tensor_tensor", out=rtot, in0=rr, in1=r_, op=ALU.mult), [s8b], [s8b])
            vop(I("tensor_scalar", out=hs_, in0=psN[:, 0:256], scalar1=rtot,
                                                                        scalar2=None, op0=ALU.mult), [pbN, s8b], [hsb])
            psT, pbT = psum(K)
            for vt in range(2):
                P.op("tensor", I("transpose", out=psT[:, vt * 128:(vt + 1) * 128],
                                                                               in_=hs_[:, vt * 128:(vt + 1) * 128],
                                                                               identity=K.ident),
                     [hsb, K.cstb], [pbT], inc=(vt == 1))
            for vt in range(2):
                ghc = pvcol(K, li, PV_GH + 2 * h + vt)
                vop(I("scalar_tensor_tensor",
                    out=tv, in0=psT[:, vt * 128:(vt + 1) * 128], scalar=ghc, in1=skx[:, vt, sl], op0=ALU.mult,
                    op1=ALU.add), [pbT, skxb, K.pvb], [tvb])
                vop(I("tensor_tensor", out=ym[:, vt, sl], in0=tv, in1=sigo[:, vt, sl],
                                                                          op=ALU.mult), [tvb, sigob], [ymb])
            if c < NT - 1:
                vop(I("tensor_scalar", out=Vg_, in0=Va[:, c, :], scalar1=gam, scalar2=None,
                                                                      op0=ALU.mult), [Vab, bgfb], [Vgb])
                for kt in range(2):
                    psU, pbU = psum(K)
                    P.op("tensor", mm(psU[:, 0:257], ktok[:, c, kt * 128:(kt + 1) * 128], Vg_, True, True),
                         [ktokb, Vgb], [pbU])
                    if c == 0:
                        P.op("scalar", I("copy", out=Cst[:, kt, :], in_=psU[:, 0:257]), [pbU], [Cstb])
                    else:
                        dcol = decb[:, h * 16 + c:h * 16 + c + 1]
                        vop(I("scalar_tensor_tensor",
                            out=Cst[:, kt, :], in0=Cst[:, kt, :], scalar=dcol, in1=psU[:, 0:257], op0=ALU.mult,
                            op1=ALU.add), [pbU, Cstb, decbb], [Cstb])
        for vt in range(2):
            ct = 2 * h + vt
            P.dma("sync", K.yT.ap()[Y_M + ct * 128:Y_M + (ct + 1) * 128, :], ym[:, vt, :], reads=[ymb],
                  writes=[dbuf(K, ("yT", (Y_M // 128) + ct))])
    A.release(m0)


def out_proj(K, li, aT, abufs, nch, wh, g, colw, W, src_t, dst_t, xst):
    P = K.P
    for cb in range(D // colw):
        wv, wb = W.load(wsrc(wh, li, 0, nch, cb * colw, colw), nch, colw)
        for tt in range(8):
            gt_ = g * 8 + tt
            xt, xb = xst.next()
            keys = xkeys(K, gt_, cb * colw, (cb + 1) * colw)
            P.dma("sync", xt[:, 0:colw], src_t.ap()[gt_ * 128:(gt_ + 1) * 128, cb * colw:(cb + 1) * colw], reads=keys,
                  writes=[xb])
            ps, pb = psum(K)
            for c in range(nch):
                P.op("tensor", mm(ps[:, 0:colw], aT[:, c, tt * 128:(tt + 1) * 128], wv[:, c, :], c == 0, c == nch - 1),
                     [wb] + abufs, [pb], inc=(c == nch - 1))
            P.op("vector", I("tensor_tensor", out=xt[:, 0:colw], in0=ps[:, 0:colw], in1=xt[:, 0:colw],
                                                                    op=ALU.add), [pb, xb], [xb])
            P.dma("sync", dst_t.ap()[gt_ * 128:(gt_ + 1) * 128, cb * colw:(cb + 1) * colw], xt[:, 0:colw], reads=[xb],
                  writes=keys)


def stage_merge(K, li):
    P, A = K.P, K.A
    src_t = K.x_in if li == 0 else K.xs
    for g in range(2):
        m0 = A.mark()
        mT, mTb = A.alloc([128, DC, 1024], BF16, nbufs=DC)
        W = WStream(K, 2, 22 * 512)
        m1 = A.mark()
        yTs, ytb = A.alloc([128, 22, 1024], BF16)
        P.dma("sync", yTs, K.yT.ap()[:, g * 1024:(g + 1) * 1024].rearrange("(c p) t -> p c t", p=128),
              reads=[dbuf(K, ("yT", c)) for c in range(22)], writes=[ytb])
        gts = [A.alloc([128, 3, 1024], BF16) for _ in range(2)]
        tmps = [[A.alloc([128, 512], F32) for _ in range(3)] for _ in range(2)]
        n = 0
        gview = K.gT.ap()[:, g * 1024:(g + 1) * 1024].rearrange("(b f p) t -> p b f t", b=3, p=128)
        for blk in range(4):
            v, b = W.slots[W.i]
            W.i = (W.i + 1) % len(W.slots)
            wv = v[:, 0:22 * 512].rearrange("p (a b) -> p a b", a=22)
            c0 = blk * 512
            P.dma("gpsimd", wv[:, 0:8, :], wsrc(K.w_br_a, li, 0, 8, c0, 512), writes=[b])
            P.dma("gpsimd", wv[:, 8:14, :], wsrc(K.w_br_s, li, 0, 6, c0, 512), writes=[b])
            P.dma("gpsimd", wv[:, 14:22, :], wsrc(K.w_br_m, li, 0, 8, c0, 512), writes=[b])
            for j in range(4):
                fb = blk * 4 + j
                gt, gtb = gts[fb % 2]
                P.dma("sync", gt, gview[:, :, fb, :], reads=[dbuf(K, ("gT", br * 16 + fb)) for br in range(3)],
                      writes=[gtb])
                for tb in range(2):
                    sl = slice(tb * 512, (tb + 1) * 512)
                    tm = tmps[n % 2]
                    n += 1
                    for br, (ca, cbb_) in enumerate(((0, 8), (8, 14), (14, 22))):
                        ps, pb = psum(K)
                        for c in range(ca, cbb_):
                            P.op("tensor", mm(ps, wv[:, c, j * 128:(j + 1) * 128], yTs[:, c, sl], c == ca, c == cbb_ - 1),
                                 [b, ytb], [pb], inc=(c == cbb_ - 1))
                        t_, t_b = tm[br]
                        P.op("vector", I("tensor_tensor",
                            out=t_, in0=ps, in1=gt[:, br, sl], op=ALU.mult), [pb, gtb], [t_b])
                    P.op("gpsimd", I("tensor_tensor", out=tm[0][0], in0=tm[0][0], in1=tm[1][0], op=ALU.add),
                         [tm[0][1], tm[1][1]], [tm[0][1]])
                    P.op("gpsimd", I("tensor_tensor", out=mT[:, fb, sl], in0=tm[0][0],
                                                                                  in1=tm[2][0], op=ALU.add),
                         [tm[0][1], tm[2][1]], [mTb[fb]])
        A.release(m1)
        xst = Stager(K, 4, [128, 512], F32)
        out_proj(K, li, mT, mTb, DC, K.w_out, g, 512, W, src_t, K.xs, xst)
        A.release(m0)


def stage_xattn(K, li):
    P, A = K.P, K.A
    mA = A.mark()
    kT, kTb = A.alloc([128, 16, 256], BF16)
    Vx, Vxb = A.alloc([128, 2, D], BF16)
    m0 = A.mark()
    W = WStream(K, 2, 16 * 512)
    sqs = [A.alloc([128, 4, 256], F32) for _ in range(1)]
    rt, rtb = A.alloc([128, 256], F32)
    memb = K.memTb
    for h in range(4):
        wv, wb = W.load(wsrc(K.w_xkv, li, 0, DC, h * 512, 512), DC, 512)
        sq, sqb = sqs[0]
        pss = []
        for et in range(4):
            ps, pb = psum(K)
            pss.append((ps, pb))
            for c in range(DC):
                P.op("tensor", mm(ps[:, 0:256], wv[:, c, et * 128:(et + 1) * 128], K.memT[:, c, :], c == 0, c == DC - 1),
                     [wb] + memb, [pb], inc=(c == DC - 1))
            P.op("scalar", I("activation", out=sq[:, et, :], in_=ps[:, 0:256], func=AF.Square),
                 [pb], [sqb])
        ps2, pb2 = psum(K)
        for et in range(4):
            P.op("tensor", mm(ps2[:, 0:256], K.ones, sq[:, et, :], et == 0, et == 3), [K.cstb, sqb], [pb2], inc=(et == 3))
        P.op("scalar", I("activation", out=rt, in_=ps2[:, 0:256], func=AF.Sqrt, scale=1.0 / 512, bias=EPS),
             [pb2], [rtb])
        P.op("vector", I("reciprocal", out=rt, in_=rt), [rtb], [rtb])
        for et in range(4):
            ps, pb = pss[et]
            g_ = K.pvd[:, li, 1 + et:2 + et]
            P.op("vector", I("scalar_tensor_tensor",
                out=kT[:, h * 4 + et, :], in0=ps[:, 0:256], scalar=g_, in1=rt, op0=ALU.mult, op1=ALU.mult),
                [pb, rtb, K.pvdb], [kTb])
    for cb in range(4):
        wv, wb = W.load(wsrc(K.w_xkv, li, 0, DC, D + cb * 512, 512), DC, 512)
        for mt in range(2):
            ps, pb = psum(K)
            for c in range(DC):
                P.op("tensor", mm(ps, K.memT[:, c, mt * 128:(mt + 1) * 128], wv[:, c, :], c == 0, c == DC - 1),
                     [wb] + memb, [pb], inc=(c == DC - 1))
            P.op("scalar", I("copy", out=Vx[:, mt, cb * 512:(cb + 1) * 512], in_=ps), [pb], [Vxb])
    A.release(m0)
    for g in range(2):
        m0 = A.mark()
        oT, oTb = A.alloc([128, DC, 1024], BF16, nbufs=DC)
        W = WStream(K, 2, 16 * 512)
        m1 = A.mark()
        hxT, hxb = A.alloc([128, DC, 1024], BF16, nbufs=8)

        def src_fn(tt, g=g):
            gt_ = g * 8 + tt
            return K.xs.ap()[gt_ * 128:(gt_ + 1) * 128, :], xkeys(K, gt_)

        norm_to_T(K, li, src_fn, 8, PV_NX, hxT, hxb)
        sq, sqb = A.alloc([128, 4, 512], F32)
        rt, rtb = A.alloc([128, 512], F32)
        qn, qnb = A.alloc([128, 4, 512], BF16)
        PT, PTb = A.alloc([128, 2, 512], BF16)
        rd, rdb = A.alloc([128, 512], F32)
        for h in range(4):
            wv, wb = W.load(wsrc(K.w_xq, li, 0, DC, h * 512, 512), DC, 512)
            for tb in range(2):
                sl = slice(tb * 512, (tb + 1) * 512)
                pss = []
                for et in range(4):
                    ps, pb = psum(K)
                    pss.append((ps, pb))
                    for c in range(DC):
                        P.op("tensor", mm(ps, wv[:, c, et * 128:(et + 1) * 128], hxT[:, c, sl], c == 0, c == DC - 1),
                             [wb] + hxb[tb * 4:tb * 4 + 4], [pb], inc=(c == DC - 1))
                    P.op("scalar", I("activation", out=sq[:, et, :], in_=ps, func=AF.Square),
                         [pb], [sqb])
                ps2, pb2 = psum(K)
                for et in range(4):
                    P.op("tensor", mm(ps2, K.ones, sq[:, et, :], et == 0, et == 3), [K.cstb, sqb], [pb2], inc=(et == 3))
                P.op("scalar", I("activation", out=rt, in_=ps2, func=AF.Sqrt, scale=1.0 / 512, bias=EPS),
                     [pb2], [rtb])
                P.op("vector", I("reciprocal", out=rt, in_=rt), [rtb], [rtb])
                for et in range(4):
                    ps, pb = pss[et]
                    g_ = pvcol(K, li, PV_GXQ + et)
                    P.op("vector", I("scalar_tensor_tensor",
                        out=qn[:, et, :], in0=ps, scalar=g_, in1=rt, op0=ALU.mult, op1=ALU.mult),
                        [pb, rtb, K.pvb], [qnb])
                for mt in range(2):
                    ps, pb = psum(K)
                    for et in range(4):
                        P.op("tensor", mm(ps, kT[:, h * 4 + et, mt * 128:(mt + 1) * 128], qn[:, et, :], et == 0, et == 3),
                             [kTb, qnb], [pb], inc=(et == 3))
                    P.op("scalar", I("activation", out=PT[:, mt, :], in_=ps, func=AF.Exp), [pb], [PTb])
                psD, pbD = psum(K)
                for mt in range(2):
                    P.op("tensor", mm(psD, K.onesb, PT[:, mt, :], mt == 0, mt == 1), [K.cbb, PTb], [pbD], inc=(mt == 1))
                P.op("vector", I("reciprocal", out=rd, in_=psD), [pbD], [rdb])
                for vt in range(4):
                    ps, pb = psum(K)
                    for mt in range(2):
                        P.op("tensor", mm(ps, Vx[:, mt, h * 512 + vt * 128:h * 512 + (vt + 1) * 128], PT[:, mt, :],
                                          mt == 0, mt == 1), [Vxb, PTb], [pb], inc=(mt == 1))
                    P.op("vector", I("tensor_tensor",
                        out=oT[:, h * 4 + vt, sl], in0=ps, in1=rd, op=ALU.mult), [pb, rdb], [oTb[h * 4 + vt]])
        A.release(m1)
        xst = Stager(K, 4, [128, 512], F32)
        out_proj(K, li, oT, oTb, DC, K.w_xo, g, 512, W, K.xs, K.xs, xst)
        A.release(m0)
    A.release(mA)


def stage_ffn(K, li):
    P, A = K.P, K.A
    dst_t = K.out if li == DEPTH - 1 else K.xs
    NF = FF // 128
    for g in range(2):
        m0 = A.mark()
        aT, aTb = A.alloc([128, NF, 1024], BF16, nbufs=NF)
        m1 = A.mark()
        hfT, hfb = A.alloc([128, DC, 1024], BF16, nbufs=8)

        def src_fn(tt, g=g):
            gt_ = g * 8 + tt
            return K.xs.ap()[gt_ * 128:(gt_ + 1) * 128, :], xkeys(K, gt_)

        norm_to_T(K, li, src_fn, 8, PV_NF, hfT, hfb)
        W = WStream(K, 4, 16 * 512)
        sgs = [A.alloc([128, 512], F32) for _ in range(2)]
        n = 0
        for blk in range(FF // 512):
            wg_, wgb_ = W.load(wsrc(K.w_gu, li, 0, DC, blk * 512, 512), DC, 512)
            wu_, wub_ = W.load(wsrc(K.w_gu, li, 0, DC, FF + blk * 512, 512), DC, 512)
            for j in range(4):
                fc = blk * 4 + j
                for tb in range(2):
                    sl = slice(tb * 512, (tb + 1) * 512)
                    psg, pbg = psum(K)
                    for c in range(DC):
                        P.op("tensor", mm(psg, wg_[:, c, j * 128:(j + 1) * 128], hfT[:, c, sl], c == 0, c == DC - 1),
                             [wgb_] + hfb[tb * 4:tb * 4 + 4], [pbg], inc=(c == DC - 1))
                    psu, pbu = psum(K)
                    for c in range(DC):
                        P.op("tensor", mm(psu, wu_[:, c, j * 128:(j + 1) * 128], hfT[:, c, sl], c == 0, c == DC - 1),
                             [wub_] + hfb[tb * 4:tb * 4 + 4], [pbu], inc=(c == DC - 1))
                    sg, sgb = sgs[n % 2]
                    n += 1
                    P.op("scalar", I("activation", out=sg, in_=psg, func=AF.Silu), [pbg], [sgb])
                    P.op("vector", I("tensor_tensor", out=aT[:, fc, sl], in0=psu, in1=sg,
                                                                                         op=ALU.mult),
                         [pbu, sgb], [aTb[fc]])
        A.release(m1)
        W2 = WStream(K, 2, NF * 256)
        xst = Stager(K, 4, [128, 256], F32)
        out_proj(K, li, aT, aTb, NF, K.w_down, g, 256, W2, K.xs, dst_t, xst)
        A.release(m0)


def host_layout(inputs):
    L = DEPTH
    f = lambda a: np.asarray(a, np.float32)
    pv = np.zeros((L, 128, NPV), np.float32)

    def colify(v):
        v = f(v)
        return v.reshape(-1, 128).T

    for li in range(L):
        pv[li, :, PV_NMIX:PV_NMIX + 16] = colify(inputs["norm_mix"][li])
        pv[li, :, PV_NX:PV_NX + 16] = colify(inputs["norm_x"][li])
        pv[li, :, PV_NF:PV_NF + 16] = colify(inputs["norm_ffn"][li])
        pv[li, :, PV_BG:PV_BG + 48] = colify(inputs["b_gate"][li])
        pv[li, :, PV_GQA:PV_GQA + 1] = colify(inputs["g_qa"][li])
        pv[li, :, PV_GKA:PV_GKA + 1] = colify(inputs["g_ka"][li])
        pv[li, :, PV_DSK:PV_DSK + 6] = colify(inputs["d_skip"][li])
        pv[li, :, PV_BGLU:PV_BGLU + 6] = colify(inputs["b_glu"][li])
        for j in range(4):
            pv[li, :, PV_CW + j * 8:PV_CW + j * 8 + 8] = colify(inputs["conv_w"][li][j])
        pv[li, :, PV_CB:PV_CB + 8] = colify(inputs["conv_b"][li])
        pv[li, :, PV_GH:PV_GH + 8] = colify(inputs["g_hm"][li])
        pv[li, :, PV_SK:PV_SK + 8] = colify(inputs["skip_m"][li])
        pv[li, :, PV_GXQ:PV_GXQ + 4] = colify(inputs["g_xq"][li])
        pv[li, :, PV_GXK:PV_GXK + 4] = colify(inputs["g_xk"][li])
        pv[li, :, PV_GMEM:PV_GMEM + 16] = colify(inputs["g_mem"])
    pg = np.stack([f(inputs["b_i"]), f(inputs["b_f"])], axis=-1)
    s5p = np.zeros((L, 128, 72), np.float32)
    s5b = np.zeros((L, 2, 128, 24 * 16), np.float32)
    s5c = np.zeros((L, 2, 128, 24 * 16), np.float32)
    for li in range(L):
        s5p[li, :, 0:24] = f(inputs["lam_re"][li]).reshape(24, 128).T
        s5p[li, :, 24:48] = f(inputs["lam_im"][li]).reshape(24, 128).T
        s5p[li, :, 48:72] = np.repeat(f(inputs["log_dt"][li]), 64).reshape(24, 128).T
        for k, nm in enumerate(("b_re", "b_im")):
            s5b[li, k] = f(inputs[nm][li]).reshape(24, 128, 16).transpose(1, 0, 2).reshape(128, 384)
        for k, nm in enumerate(("c_re", "c_im")):
            s5c[li, k] = f(inputs[nm][li]).reshape(24, 2, 16, 64).transpose(1, 3, 0, 2).reshape(128, 384)
    return dict(pvec=pv, pgate=np.ascontiguousarray(pg), s5p=s5p, s5b=s5b, s5c=s5c, consts=host_consts())


BIG = ("w_in", "rel_bias", "w_glu", "wq_m", "wk_m", "w_br_a", "w_br_s", "w_br_m", "w_out", "w_xq", "w_xkv", "w_xo",
       "w_gu", "w_down")

_NC_CACHE = {}


def make_in_maps(inputs, ncores=8):
    small = host_layout(inputs)
    shared = {k: np.ascontiguousarray(np.asarray(inputs[k], np.float32)) for k in BIG}
    shared.update(small)
    maps = []
    for c in range(ncores):
        b = c % 4
        m = dict(shared)
        m["x"] = np.ascontiguousarray(np.asarray(inputs["x"][b], np.float32))
        m["mem"] = np.ascontiguousarray(np.asarray(inputs["mem"][b], np.float32))
        maps.append(m)
    return maps


def kernel(**inputs):
    if "nc" not in _NC_CACHE:
        _NC_CACHE["nc"] = build_nc()
    nc = _NC_CACHE["nc"]
    maps = make_in_maps(inputs, 8)
    res = run_bass_kernel_spmd(nc, maps, core_ids=list(range(8)))
    out = np.stack([np.asarray(res.results[b]["out"], np.float32) for b in range(4)], axis=0)
    return out
```
